# Optimizing a Trainium2 kernel written in Bass

```python
import math
import jax, jax.numpy as jnp
from jax import lax
import numpy as np

D_MODEL = 2048
BATCH = 2
SEQ = 16384
DEPTH = 2

CHUNK = 64
Q_BLOCK = 128
HEAD_DIM = 64
D_MIX = D_MODEL
N_GROUPS = 4
GROUP_W = D_MIX // N_GROUPS
SB_HEADS = GROUP_W // HEAD_DIM
SB_WINDOW = 1024
DIFF_HEADS = GROUP_W // (2 * HEAD_DIM)
DIFF_WINDOW = 1024
LRU_BLOCKS = 8
LRU_BW = GROUP_W // LRU_BLOCKS
LRU_C = 8.0
LRU_CONV = 4
SC_CONV = 3
D_FF = 2 * D_MODEL
FFN_CONV = 3
N_BUCKETS = 32
MAX_DISTANCE = 128
N_IN_BLOCKS = 11
IN_COLS = N_IN_BLOCKS * GROUP_W
NORM_EPS = 1e-6
NEG_INF = -1e30

kernel_name = "hybrid_parallel_group_stream_encoder"


def rms_norm(x, g):
    xf = x.astype(jnp.float32)
    y = xf * lax.rsqrt(jnp.mean(xf * xf, axis=-1, keepdims=True) + NORM_EPS)
    return (y * g.astype(jnp.float32)).astype(x.dtype)


def causal_dwconv(x, w):
    K = w.shape[0]
    S = x.shape[1]
    xp = jnp.pad(x, ((0, 0), (K - 1, 0), (0, 0)))
    y = xp[:, 0:S] * w[0]
    for k in range(1, K):
        y = y + xp[:, k:k + S] * w[k]
    return y


def t5_bucket(rel):
    nb = N_BUCKETS // 2
    max_exact = nb // 2
    ret = jnp.where(rel > 0, nb, 0)
    n = jnp.abs(rel)
    large = max_exact + (jnp.log(jnp.maximum(n, 1).astype(jnp.float32) / max_exact)
                         / math.log(MAX_DISTANCE / max_exact) * (nb - max_exact)).astype(jnp.int32)
    large = jnp.minimum(large, nb - 1)
    return ret + jnp.where(n < max_exact, n, large)


def stick_breaking_attention(q, k, v):
    Bsz, H, S, d = q.shape
    nb = S // Q_BLOCK
    span = SB_WINDOW + Q_BLOCK
    nkb = span // Q_BLOCK
    scale = d ** -0.5
    kp = jnp.pad(k, ((0, 0), (0, 0), (SB_WINDOW, 0), (0, 0)))
    vp = jnp.pad(v, ((0, 0), (0, 0), (SB_WINDOW, 0), (0, 0)))
    rel = (jnp.arange(span) - SB_WINDOW)[None, :] - jnp.arange(Q_BLOCK)[:, None]
    band = (rel < 0) & (rel >= -SB_WINDOW)
    tri = (jnp.arange(Q_BLOCK)[:, None] > jnp.arange(Q_BLOCK)[None, :]).astype(jnp.float32)
    qb = q.reshape(Bsz, H, nb, Q_BLOCK, d).transpose(2, 0, 1, 3, 4)

    def block(args):
        qi, i = args
        ki = lax.dynamic_slice_in_dim(kp, i * Q_BLOCK, span, axis=2)
        vi = lax.dynamic_slice_in_dim(vp, i * Q_BLOCK, span, axis=2)
        kpos = i * Q_BLOCK - SB_WINDOW + jnp.arange(span)
        allowed = band & (kpos >= 0)[None, :]
        z = jnp.einsum('bhqd,bhkd->bhqk', qi, ki).astype(jnp.float32) * scale
        log_keep = jnp.where(allowed, jax.nn.log_sigmoid(-z), 0.0)
        lk = log_keep.reshape(Bsz, H, Q_BLOCK, nkb, Q_BLOCK)
        within = jnp.einsum('bhqnj,js->bhqns', lk, tri)
        tot = jnp.sum(lk, axis=-1)
        later_blocks = lax.cumsum(tot, axis=3, reverse=True) - tot
        after = (within + later_blocks[..., None]).reshape(Bsz, H, Q_BLOCK, span)
        w = jnp.where(allowed, jnp.exp(jax.nn.log_sigmoid(z) + after), 0.0)
        return jnp.einsum('bhqk,bhkd->bhqd', w.astype(vi.dtype), vi)

    out = lax.map(block, (qb, jnp.arange(nb)))
    return out.transpose(1, 2, 0, 3, 4).reshape(Bsz, H, S, d)


def differential_attention(q, k, v, rel_bias, lam, subln_g, lam_init):
    Bsz, H, _, S, d = q.shape
    nb = S // Q_BLOCK
    span = DIFF_WINDOW + Q_BLOCK
    scale = d ** -0.5
    kp = jnp.pad(k, ((0, 0), (0, 0), (0, 0), (DIFF_WINDOW, 0), (0, 0)))
    vp = jnp.pad(v, ((0, 0), (0, 0), (DIFF_WINDOW, 0), (0, 0)))
    a_key = jnp.arange(span) - DIFF_WINDOW
    a_qry = jnp.arange(Q_BLOCK)
    rel = a_key[None, :] - a_qry[:, None]
    bias = rel_bias.astype(jnp.float32)[t5_bucket(rel)]
    bias = bias.transpose(2, 0, 1)[None, :, None]
    kc = (a_key // CHUNK)[None, :]
    qc = (a_qry // CHUNK)[:, None]
    band = (kc <= qc) & (kc >= qc - DIFF_WINDOW // CHUNK)
    qb = q.reshape(Bsz, H, 2, nb, Q_BLOCK, d).transpose(3, 0, 1, 2, 4, 5)

    def block(args):
        qi, i = args
        ki = lax.dynamic_slice_in_dim(kp, i * Q_BLOCK, span, axis=3)
        vi = lax.dynamic_slice_in_dim(vp, i * Q_BLOCK, span, axis=2)
        kpos = i * Q_BLOCK - DIFF_WINDOW + jnp.arange(span)
        allowed = band & (kpos >= 0)[None, :]
        logits = jnp.einsum('bhmqd,bhmkd->bhmqk', qi, ki).astype(jnp.float32) * scale
        p = jax.nn.softmax(jnp.where(allowed, logits + bias, NEG_INF), axis=-1)
        attn = p[:, :, 0] - lam * p[:, :, 1]
        return jnp.einsum('bhqk,bhke->bhqe', attn.astype(vi.dtype), vi)

    out = lax.map(block, (qb, jnp.arange(nb)))
    out = out.transpose(1, 2, 0, 3, 4).reshape(Bsz, H, S, 2 * d)
    return rms_norm(out, subln_g) * (1.0 - lam_init)


def rg_lru_branch(xb, gate, conv_w, conv_b, w_gate, b_gate, lam):
    Bsz, S, W = xb.shape
    xc = (causal_dwconv(xb, conv_w) + conv_b).astype(jnp.float32)
    xblk = xc.reshape(Bsz, S, LRU_BLOCKS, LRU_BW)
    g = jnp.einsum('bsnc,gncd->gbsnd', xblk, w_gate.astype(jnp.float32)).reshape(2, Bsz, S, W)
    g = jax.nn.sigmoid(g + b_gate.astype(jnp.float32)[:, None, None, :])
    r, i = g[0], g[1]
    log_a = -LRU_C * r * jax.nn.softplus(-lam.astype(jnp.float32))
    a = jnp.exp(log_a)
    b = jnp.sqrt(-jnp.expm1(2.0 * log_a)) * (i * xc)

    def combine(left, right):
        a1, b1 = left
        a2, b2 = right
        return a1 * a2, a2 * b1 + b2

    _, h = lax.associative_scan(combine, (a, b), axis=1)
    return (h * jax.nn.gelu(gate.astype(jnp.float32), approximate=True)).astype(xb.dtype)


def conv_geglu_ffn(h, w_up, conv_w, w_down):
    u = causal_dwconv(h @ w_up, conv_w)
    g, up = jnp.split(u, 2, axis=-1)
    return (jax.nn.gelu(g, approximate=True) * up) @ w_down


def setup_inputs(seed: int = 0) -> dict:
    key = jax.random.key(seed)
    ks = jax.random.split(key, 17)
    f32 = jnp.float32
    nrm = lambda k, s: jax.random.normal(k, s, dtype=f32)
    u = jax.random.uniform(ks[11], (DEPTH, GROUP_W), dtype=f32, minval=0.9, maxval=0.999)
    a0 = u ** (1.0 / LRU_C)
    return {
        "x": nrm(ks[0], (BATCH, SEQ, D_MODEL)),
        "norm_gains": 1.0 + 0.01 * nrm(ks[1], (DEPTH, 4, D_MODEL)),
        "w_in": nrm(ks[2], (DEPTH, D_MODEL, IN_COLS)) * D_MODEL ** -0.5,
        "w_out": nrm(ks[3], (DEPTH, D_MIX, D_MODEL)) * D_MIX ** -0.5,
        "rel_bias": 0.5 * nrm(ks[4], (N_BUCKETS, DIFF_HEADS)),
        "diff_lambda": 0.1 * nrm(ks[5], (DEPTH, 4, HEAD_DIM)),
        "diff_subln_g": 1.0 + 0.01 * nrm(ks[6], (DEPTH, 2 * HEAD_DIM)),
        "lru_conv_w": nrm(ks[7], (DEPTH, LRU_CONV, GROUP_W)) * LRU_CONV ** -0.5,
        "lru_conv_b": 0.01 * nrm(ks[8], (DEPTH, GROUP_W)),
        "lru_w_gate": nrm(ks[9], (DEPTH, 2, LRU_BLOCKS, LRU_BW, LRU_BW)) * LRU_BW ** -0.5,
        "lru_b_gate": 0.01 * nrm(ks[10], (DEPTH, 2, GROUP_W)),
        "lru_lambda": jnp.log(a0) - jnp.log1p(-a0),
        "sc_conv_w": nrm(ks[12], (DEPTH, SC_CONV, GROUP_W)) * SC_CONV ** -0.5,
        "ffn_w_up": nrm(ks[13], (DEPTH, D_MODEL, 2 * D_FF)) * D_MODEL ** -0.5,
        "ffn_conv_w": nrm(ks[14], (DEPTH, FFN_CONV, 2 * D_FF)) * FFN_CONV ** -0.5,
        "ffn_w_down": nrm(ks[15], (DEPTH, D_FF, D_MODEL)) * D_FF ** -0.5,
    }


def reference(x, norm_gains, w_in, w_out, rel_bias, diff_lambda, diff_subln_g,
              lru_conv_w, lru_conv_b, lru_w_gate, lru_b_gate, lru_lambda,
              sc_conv_w, ffn_w_up, ffn_conv_w, ffn_w_down):
    Bsz, S, _ = x.shape
    for l in range(DEPTH):
        lam_init = 0.8 - 0.6 * math.exp(-0.3 * l)
        h = rms_norm(x, norm_gains[l, 0])
        (sb_q, sb_k, sb_v, df_q, df_k, df_v,
         lru_x, lru_g, sc_b, sc_c, sc_x) = jnp.split(h @ w_in[l], N_IN_BLOCKS, axis=-1)

        to_heads = lambda t: t.reshape(Bsz, S, SB_HEADS, HEAD_DIM).transpose(0, 2, 1, 3)
        y_sb = stick_breaking_attention(to_heads(sb_q), to_heads(sb_k), to_heads(sb_v))
        y_sb = y_sb.transpose(0, 2, 1, 3).reshape(Bsz, S, GROUP_W)

        to_pairs = lambda t: t.reshape(Bsz, S, DIFF_HEADS, 2, HEAD_DIM).transpose(0, 2, 3, 1, 4)
        lv = diff_lambda[l].astype(jnp.float32)
        lam = jnp.exp(jnp.sum(lv[0] * lv[1])) - jnp.exp(jnp.sum(lv[2] * lv[3])) + lam_init
        dv = df_v.reshape(Bsz, S, DIFF_HEADS, 2 * HEAD_DIM).transpose(0, 2, 1, 3)
        y_df = differential_attention(to_pairs(df_q), to_pairs(df_k), dv, rel_bias, lam,
                                      diff_subln_g[l], lam_init)
        y_df = y_df.transpose(0, 2, 1, 3).reshape(Bsz, S, GROUP_W)

        y_lru = rg_lru_branch(lru_x, lru_g, lru_conv_w[l], lru_conv_b[l],
                              lru_w_gate[l], lru_b_gate[l], lru_lambda[l])

        y_sc = sc_b * causal_dwconv(sc_c * sc_x, sc_conv_w[l])

        mixed = jnp.concatenate([y_sb, y_df.astype(x.dtype), y_lru, y_sc], axis=-1) @ w_out[l]
        x = x + rms_norm(mixed, norm_gains[l, 1])
        h = rms_norm(x, norm_gains[l, 2])
        x = x + rms_norm(conv_geglu_ffn(h, ffn_w_up[l], ffn_conv_w[l], ffn_w_down[l]),
                         norm_gains[l, 3])
    return x
```

```python
import math
import types
import contextlib
import numpy as np
import concourse.bass as bass
import concourse.mybir as mybir
from concourse.bass_utils import run_bass_kernel_spmd

F32 = mybir.dt.float32
BF16 = mybir.dt.bfloat16
AF = mybir.ActivationFunctionType
ALU = mybir.AluOpType

D = 2048
NCH = 16
T = 512
GW = 512
DFF = 4096
INC = 5632
EPS = 1e-6
NEG = -30000.0
PROFILE = False
DBG_NOFP32 = False
DBG_SEQ = False


def _freeze(fn, depth=0):
    if not isinstance(fn, types.FunctionType) or fn.__closure__ is None or depth > 3:
        return fn
    cells = []
    for c in fn.__closure__:
        try:
            v = c.cell_contents
        except ValueError:
            cells.append(c)
            continue
        if isinstance(v, types.FunctionType) and v.__name__ == "<lambda>":
            v = _freeze(v, depth + 1)
        cells.append(types.CellType(v))
    return types.FunctionType(fn.__code__, fn.__globals__, fn.__name__, fn.__defaults__, tuple(cells))


class Buf:
    __slots__ = ("name", "lw", "rd")

    def __init__(self, name):
        self.name = name
        self.lw = None
        self.rd = {}


class Prog:
    ENGS = ("pe", "dve", "act", "pool", "sp")
    NSLOT = 8

    def __init__(self, nc):
        self.nc = nc
        self.ops = {e: [] for e in self.ENGS}
        self.stack = contextlib.ExitStack()
        self.sem = {e: self.stack.enter_context(nc.semaphore("s_" + e)) for e in self.ENGS}
        self.cnt = {e: 0 for e in self.ENGS}
        self.waited = {e: {} for e in self.ENGS}
        self.signals = {e: set() for e in self.ENGS}
        self.pending = {e: [] for e in self.ENGS}
        self.dq = {}
        for q in ("sp", "pool"):
            self.dq[q] = dict(n=0, sems=[self.stack.enter_context(nc.semaphore("d_%s%d" % (q, i)))
                                         for i in range(self.NSLOT)])
        self.final = {e: [] for e in self.ENGS}
        self.scope = None
        self.opscope = {e: [] for e in self.ENGS}

    def _collect(self, eng, reads, writes):
        deps = []
        for b in reads:
            if b.lw is not None:
                deps.append(b.lw)
        for b in writes:
            if b.lw is not None and not (b.lw[0] == "e" and b.lw[1] == eng):
                deps.append(b.lw)
            for tk in b.rd.values():
                if not (tk[0] == "e" and tk[1] == eng):
                    deps.append(tk)
        if eng == "pe":
            deps = [d for d in deps if not (d[0] == "e" and d[1] == "pe")]
        deps.extend(self.pending[eng])
        self.pending[eng] = []
        need = {}
        for tk in deps:
            k = (tk[0], tk[1])
            if k not in need or need[k] < tk[2]:
                need[k] = tk[2]
        out = []
        w = self.waited[eng]
        for k, v in need.items():
            if w.get(k, 0) < v:
                w[k] = v
                out.append((k[0], k[1], v))
                if k[0] == "e":
                    self.signals[k[1]].add(v)
        return out

    def _update(self, tok, rkey, reads, writes):
        for b in reads:
            b.rd[rkey] = tok
        for b in writes:
            b.lw = tok
            b.rd = {}

    def op(self, eng, fn, reads=(), writes=()):
        waits = self._collect(eng, reads, writes)
        self.cnt[eng] += 1
        tok = ("e", eng, self.cnt[eng])
        self.ops[eng].append((waits, _freeze(fn), ("e", self.cnt[eng])))
        self.opscope[eng].append(self.scope)
        self._update(tok, ("e", eng), reads, writes)
        return tok

    def dma(self, q, out, in_, reads=(), writes=(), final=False):
        d = self.dq[q]
        i = d["n"]
        d["n"] += 1
        slot = i % self.NSLOT
        val = 16 * (i // self.NSLOT + 1)
        waits = self._collect(q, reads, writes)
        key = (q, slot)
        if i >= self.NSLOT:
            w = self.waited[q]
            if w.get(("d", key), 0) < val - 16:
                w[("d", key)] = val - 16
                waits.append(("d", key, val - 16))
        tok = ("d", key, val)
        self.ops[q].append((waits, (lambda e, o=out, i_=in_: e.dma_start(out=o, in_=i_)), ("d", slot)))
        self.opscope[q].append(self.scope)
        self._update(tok, ("d", key), reads, writes)
        if final:
            self.final[q].append(tok)
        return tok

    def fence(self, bufs):
        toks = []
        for b in bufs:
            if b.lw is not None:
                toks.append(b.lw)
            toks.extend(b.rd.values())
        for e in ("pe", "dve", "act", "pool"):
            self.pending[e].extend([tk for tk in toks if not (tk[0] == "e" and tk[1] == e)])

    def barrier(self):
        toks = [("e", e, self.cnt[e]) for e in self.ENGS if self.cnt[e] > 0]
        for q, d in self.dq.items():
            n = d["n"]
            for slot in range(min(n, self.NSLOT)):
                last_i = ((n - 1 - slot) // self.NSLOT) * self.NSLOT + slot
                toks.append(("d", (q, slot), 16 * (last_i // self.NSLOT + 1)))
        for e in self.ENGS:
            self.pending[e].extend(toks)

    def emit(self):
        nc = self.nc
        rank = {}
        for e in self.ENGS:
            rank[e] = {idx: r + 1 for r, idx in enumerate(sorted(self.signals[e]))}

        def replay(name, e):
            own = self.sem[name]
            sig = self.signals[name]
            cur = None
            for oi, (waits, fn, info) in enumerate(self.ops[name]):
                sc = self.opscope[name][oi]
                if sc != cur:
                    if cur is not None:
                        nc.leave_named_scope(cur, sid, False)
                    if sc is not None:
                        sid, _ = nc.enter_named_scope(sc, False)
                    cur = sc
                for (kind, key, v) in waits:
                    if kind == "e":
                        e.wait_ge(self.sem[key], rank[key][v])
                    else:
                        e.wait_ge(self.dq[key[0]]["sems"][key[1]], v)
                ins = fn(e)
                if info[0] == "e":
                    if info[1] in sig:
                        ins.then_inc(own, 1)
                else:
                    ins.then_inc(self.dq[name]["sems"][info[1]], 16)
            if cur is not None:
                nc.leave_named_scope(cur, sid, False)
            for (kind, key, v) in self.final[name]:
                e.wait_ge(self.dq[key[0]]["sems"][key[1]], v)

        with nc.Block() as block:
            @block.tensor
            def _(e):
                replay("pe", e)

            @block.vector
            def _(e):
                replay("dve", e)

            @block.scalar
            def _(e):
                replay("act", e)

            @block.gpsimd
            def _(e):
                replay("pool", e)

            @block.sync
            def _(e):
                replay("sp", e)
        self.stack.close()


def win_groups():
    g = []
    for name, base in (("q_sb", 0), ("k_sb", 512)):
        for c2 in range(2):
            g.append((name, c2, [base + 256 * c2, base + 256 * c2 + 128]))
    for c2 in range(2):
        g.append(("v_sb", c2, [1024 + 256 * c2, 1024 + 256 * c2 + 128]))
    for name, base in (("q_df", 1536), ("k_df", 2048)):
        for c2 in range(2):
            g.append((name, c2, [base + 256 * c2, base + 256 * c2 + 128]))
    for c2 in range(2):
        g.append(("v_df", c2, [2560 + 256 * c2, 2560 + 256 * c2 + 128]))
    for c in range(4):
        g.append(("lru", c, [3072 + 128 * c, 3584 + 128 * c]))
    for c in range(4):
        g.append(("sc_cx", c, [4608 + 128 * c, 5120 + 128 * c]))
    for c2 in range(2):
        g.append(("sc_b", c2, [4096 + 256 * c2, 4096 + 256 * c2 + 128]))
    return g


WIN_G = win_groups()
NG_IN, NG_OUT, NG_UP, NG_DN = 22, 8, 32, 16


class PPL:
    def __init__(self, L):
        o = 0
        self.g = o; o += L * 64
        self.lcw = o; o += L * 16
        self.lcb = o; o += L * 4
        self.bg = o; o += L * 8
        self.ll = o; o += L * 4
        self.scw = o; o += L * 12
        self.fcw = o; o += L * 192
        self.sub = o; o += L
        self.bfar = o; o += 4
        self.n = o


def build(S, L):
    NT = S // T
    nc = bass.Bass("TRN2", target_bir_lowering=False)
    ppl = PPL(L)
    dt = nc.dram_tensor
    xin = dt("xT", [D, S], F32, kind="ExternalInput").ap()
    w_in = dt("w_in", [L * D, INC], F32, kind="ExternalInput").ap()
    w_out = dt("w_out", [L * D, D], F32, kind="ExternalInput").ap()
    w_up = dt("w_up", [L * D, 2 * DFF], F32, kind="ExternalInput").ap()
    w_dn = dt("w_dn", [L * DFF, D], F32, kind="ExternalInput").ap()
    pp_d = dt("pp", [128, ppl.n], F32, kind="ExternalInput").ap()
    lam_d = dt("lamraw", [128, L * 256], F32, kind="ExternalInput").ap()
    bdg_d = dt("bdg", [128, L * 1024], F32, kind="ExternalInput").ap()
    cst_d = dt("consts", [128, 896], F32, kind="ExternalInput").ap()
    bias_d = dt("biasT", [128, 1024], F32, kind="ExternalInput").ap()
    yout = dt("yT", [D, S], F32, kind="ExternalOutput").ap()
    wg_in = dt("wg_in", [L * NG_IN * 128, 4096], BF16).ap()
    wg_out = dt("wg_out", [L * NG_OUT * 128, 4096], BF16).ap()
    wg_up = dt("wg_up", [L * NG_UP * 128, 4096], BF16).ap()
    wg_dn = dt("wg_dn", [L * NG_DN * 128, 4096], BF16).ap()
    xs = [xin]
    for l in range(1, L):
        xs.append(dt("xmid%d" % l, [D, S], F32).ap())
    xs.append(yout)

    st = contextlib.ExitStack()
    sb = lambda n, w, d=F32: st.enter_context(nc.sbuf_tensor(n, [128, w], d))
    xt = sb("xt", NCH * T)
    regA = sb("regA", NCH * T, BF16)
    mixed = sb("mixed", NCH * T)
    actp = sb("actp", 16 * T)
    act = actp[:, :].bitcast(BF16)
    NRB = 12 * 128
    kr_sb = sb("kr_sb", 4 * NRB, BF16)
    kr_df = sb("kr_df", 4 * NRB, BF16)
    vr_sb = sb("vr_sb", 12 * 512, BF16)
    vr_df = sb("vr_df", 12 * 512, BF16)
    NWB = 3
    wbuf = [sb("wbuf%d" % i, 4096, BF16) for i in range(NWB)]
    rstd = sb("rstd", T)
    qT = mixed[:, 0:2048].bitcast(BF16)
    convm = mixed[:, 2048:4096]
    tA = [mixed[:, 4096:4608]] * 2
    tB = [mixed[:, 4608:5120]] * 2
    tC = [mixed[:, 5120:5632]] * 2
    tD = [mixed[:, 5632:6144]] * 2
    ubg = [mixed[:, 6144:6144 + T + 2]] * 2
    ubu = [mixed[:, 6672:6672 + T + 2]] * 2
    ycs = actp[:, 0:2048].bitcast(BF16)
    Et, tt, cs32, Lp, wt = [], [], [], [], []
    for X in range(2):
        o = 2048 + X * 3072
        Et.append([actp[:, o:o + 512], actp[:, o + 512:o + 1024]])
        tt.append(actp[:, o + 1024:o + 1536])
        cs32.append(actp[:, o + 1536:o + 2048])
        Lp.append([actp[:, o + 2048:o + 2304].bitcast(BF16), actp[:, o + 2304:o + 2560].bitcast(BF16)])
        wt.append([actp[:, o + 2560:o + 2816].bitcast(BF16), actp[:, o + 2816:o + 3072].bitcast(BF16)])
    xcb = [sb("xcb_t", T, BF16)] * 2
    lxb = [sb("lxb_t", T + 4)] * 2
    zeros_bf = sb("zeros_bf", T, BF16)
    nof32 = sb("nof32", 128)
    pp = sb("pp_sb", ppl.n)
    lamr = xt[:, 3072:3072 + L * 256]
    lamt = sb("lamt", 64)
    small = sb("small", 64)
    sp8 = sb("sp8", L * 4)
    sp16 = sb("sp16", L * 4)
    neglam = sb("neglam", L)
    gsl = sb("gsl", L)
    cstf = xt[:, 2048:2048 + 896]
    cbf = sb("cbf", 768, BF16)
    btab = sb("btab", 1024)
    bdgb = sb("bdgb", L * 1024, BF16)
    lru_state = sb("lru_state", 4)
    lru_halo = sb("lru_halo", 16)
    sc_halo = sb("sc_halo", 8)
    ffn_halo = sb("ffn_halo", 128)
    ps = st.enter_context(nc.psum_tensor("ps", [128, 4096], F32))

    P = Prog(nc)
    B = {}

    def bf(name):
        if name not in B:
            B[name] = Buf(name)
        return B[name]

    RA = [bf("regA%d" % c_) for c_ in range(NCH)]
    bank = lambda b: ps[:, b * 512:(b + 1) * 512]
    bbank = lambda b: bf("bank%d" % b)
    negtri = cbf[:, 0:128]
    negones = cbf[:, 128:256]
    ones = cbf[:, 256:384]
    mdiag = cbf[:, 384:512]
    medge = cbf[:, 512:640]
    negtri_i = cbf[:, 640:768]
    col = lambda t_, c: t_[:, c:c + 1]

    P.dma("sp", pp[:], pp_d, writes=[bf("pp")])
    P.dma("sp", lamr[:], lam_d, writes=[bf("lamr")])
    P.dma("sp", cstf[:], cst_d, writes=[bf("cstf")])
    P.dma("sp", btab[:], bias_d, writes=[bf("btab")])
    P.dma("sp", xt[:, 0:L * 1024], bdg_d, writes=[bf("xt")])
    P.op("dve", lambda e: e.tensor_copy(out=cbf[:], in_=cstf[:, 0:768]), reads=[bf("cstf")], writes=[bf("cbf")])
    P.op("pool", lambda e: e.memset(zeros_bf[:], 0.0), writes=[bf("zeros")])
    P.op("pool", lambda e: e.memset(nof32[:], -1.0), writes=[bf("nof32")])
    P.op("dve", lambda e: e.tensor_copy(out=bdgb[:], in_=xt[:, 0:L * 1024]), reads=[bf("xt")], writes=[bf("bdgb")])
    for h in range(4):
        P.op("dve", lambda e, h=h: e.tensor_tensor(out=btab[:, h * 256:h * 256 + 128], in0=btab[:, h * 256:h * 256 + 128],
                                                   in1=cstf[:, 768:896], op=ALU.add),
             reads=[bf("cstf"), bf("btab")], writes=[bf("btab")])
    for l in range(L):
        lam_init = 0.8 - 0.6 * math.exp(-0.3 * l)
        b0 = l * 256
        P.op("dve", lambda e, b0=b0: e.tensor_tensor(out=lamt[:, 0:64], in0=lamr[:, b0:b0 + 64], in1=lamr[:, b0 + 64:b0 + 128], op=ALU.mult),
             reads=[bf("lamr")], writes=[bf("lamt")])
        P.op("dve", lambda e: e.reduce_sum(out=small[:, 0:1], in_=lamt[:, 0:64], axis=mybir.AxisListType.X),
             reads=[bf("lamt")], writes=[bf("small")])
        P.op("dve", lambda e, b0=b0: e.tensor_tensor(out=lamt[:, 0:64], in0=lamr[:, b0 + 128:b0 + 192], in1=lamr[:, b0 + 192:b0 + 256], op=ALU.mult),
             reads=[bf("lamr"), bf("small")], writes=[bf("lamt")])
        P.op("dve", lambda e: e.reduce_sum(out=small[:, 1:2], in_=lamt[:, 0:64], axis=mybir.AxisListType.X),
             reads=[bf("lamt")], writes=[bf("small")])
        P.op("act", lambda e: e.activation(out=small[:, 2:4], in_=small[:, 0:2], func=AF.Exp), reads=[bf("small")], writes=[bf("small")])
        P.op("dve", lambda e: e.tensor_tensor(out=small[:, 4:5], in0=small[:, 3:4], in1=small[:, 2:3], op=ALU.subtract),
             reads=[bf("small")], writes=[bf("small")])
        P.op("dve", lambda e, l=l, li=lam_init: e.tensor_scalar(out=neglam[:, l:l + 1], in0=small[:, 4:5], scalar1=-li, scalar2=None, op0=ALU.add),
             reads=[bf("small")], writes=[bf("neglam")])
        P.op("dve", lambda e, l=l, li=lam_init: e.tensor_scalar(out=gsl[:, l:l + 1], in0=pp[:, ppl.sub + l:ppl.sub + l + 1], scalar1=1.0 - li, scalar2=None, op0=ALU.mult),
             reads=[bf("pp")], writes=[bf("gsl")])
    n4 = L * 4
    lamc = pp[:, ppl.ll:ppl.ll + n4]
    s_ = lambda i: small[:, 8 + i * n4:8 + (i + 1) * n4]
    bs = bf("small2")
    P.op("dve", lambda e: e.tensor_scalar(out=s_(0), in0=lamc, scalar1=-1.0, scalar2=None, op0=ALU.mult), reads=[bf("pp")], writes=[bs])
    P.op("dve", lambda e: e.tensor_tensor(out=s_(1), in0=lamc, in1=s_(0), op=ALU.min), reads=[bf("pp"), bs], writes=[bs])
    P.op("act", lambda e: e.activation(out=s_(2), in_=s_(1), func=AF.Exp), reads=[bs], writes=[bs])
    P.op("dve", lambda e: e.tensor_scalar(out=s_(3), in0=s_(2), scalar1=2.0, scalar2=None, op0=ALU.add), reads=[bs], writes=[bs])
    P.op("dve", lambda e: e.reciprocal(out=s_(3), in_=s_(3)), reads=[bs], writes=[bs])
    P.op("dve", lambda e: e.tensor_tensor(out=s_(3), in0=s_(3), in1=s_(2), op=ALU.mult), reads=[bs], writes=[bs])
    P.op("dve", lambda e: e.tensor_tensor(out=s_(4), in0=s_(3), in1=s_(3), op=ALU.mult), reads=[bs], writes=[bs])
    P.op("dve", lambda e: e.tensor_scalar(out=s_(5), in0=s_(4), scalar1=1.0 / 11, scalar2=1.0 / 9, op0=ALU.mult, op1=ALU.add), reads=[bs], writes=[bs])
    for cst in (1.0 / 7, 1.0 / 5, 1.0 / 3, 1.0):
        P.op("dve", lambda e: e.tensor_tensor(out=s_(5), in0=s_(5), in1=s_(4), op=ALU.mult), reads=[bs], writes=[bs])
        P.op("dve", lambda e, cst=cst: e.tensor_scalar(out=s_(5), in0=s_(5), scalar1=cst, scalar2=None, op0=ALU.add), reads=[bs], writes=[bs])
    P.op("dve", lambda e: e.tensor_tensor(out=s_(5), in0=s_(5), in1=s_(3), op=ALU.mult), reads=[bs], writes=[bs])
    P.op("dve", lambda e: e.tensor_scalar(out=s_(0), in0=s_(0), scalar1=0.0, scalar2=None, op0=ALU.max), reads=[bs], writes=[bs])
    P.op("dve", lambda e: e.scalar_tensor_tensor(out=s_(1), in0=s_(5), scalar=2.0, in1=s_(0), op0=ALU.mult, op1=ALU.add), reads=[bs], writes=[bs])
    P.op("dve", lambda e: e.tensor_scalar(out=sp8[:], in0=s_(1), scalar1=-8.0, scalar2=None, op0=ALU.mult), reads=[bs], writes=[bf("sp8")])
    P.op("dve", lambda e: e.tensor_scalar(out=sp16[:], in0=s_(1), scalar1=-16.0, scalar2=None, op0=ALU.mult), reads=[bs], writes=[bf("sp8")])
    P.barrier()

    cast_engs = ("act", "dve", "pool")
    ucount = [0]

    def cast_unit(src_ap, dst_ap, n, cw):
        u = ucount[0]
        ucount[0] += 1
        s = u % 8
        fs = mixed[:, s * 1024:s * 1024 + n]
        fb = act[:, s * 1024:s * 1024 + n]
        P.dma("sp", fs.rearrange("p (k c) -> p k c", c=cw), src_ap, writes=[bf("stg%d" % s)])
        eng = cast_engs[u % 3]
        if eng == "act":
            P.op("act", lambda e: e.activation(out=fb, in_=fs, func=AF.Copy), reads=[bf("stg%d" % s)], writes=[bf("stb%d" % s)])
        else:
            P.op(eng, lambda e: e.tensor_copy(out=fb, in_=fs), reads=[bf("stg%d" % s)], writes=[bf("stb%d" % s)])
        P.dma("pool", dst_ap, fb.rearrange("p (k c) -> p k c", c=cw), reads=[bf("stb%d" % s)], writes=[bf("wscr")])

    for l in range(L):
        for gi, (kind, idx, cols) in enumerate(WIN_G):
            for kq in range(4):
                for pi, c0 in enumerate(cols):
                    src = w_in[l * D + kq * 512:l * D + (kq + 1) * 512, c0:c0 + 128].rearrange("(k p) c -> p k c", p=128)
                    r0 = (l * NG_IN + gi) * 128
                    dst = wg_in[r0:r0 + 128, :].rearrange("p (k c) -> p k c", c=256)[:, kq * 4:(kq + 1) * 4, pi * 128:(pi + 1) * 128]
                    cast_unit(src, dst, 512, 128)
        for gi in range(NG_OUT):
            for kq in range(4):
                src = w_out[l * D + kq * 512:l * D + (kq + 1) * 512, gi * 256:(gi + 1) * 256].rearrange("(k p) c -> p k c", p=128)
                r0 = (l * NG_OUT + gi) * 128
                dst = wg_out[r0:r0 + 128, kq * 1024:(kq + 1) * 1024].rearrange("p (k c) -> p k c", c=256)
                cast_unit(src, dst, 1024, 256)
        for gi in range(NG_UP):
            for kq in range(4):
                for pi, c0 in enumerate((gi * 128, DFF + gi * 128)):
                    src = w_up[l * D + kq * 512:l * D + (kq + 1) * 512, c0:c0 + 128].rearrange("(k p) c -> p k c", p=128)
                    r0 = (l * NG_UP + gi) * 128
                    dst = wg_up[r0:r0 + 128, :].rearrange("p (k c) -> p k c", c=256)[:, kq * 4:(kq + 1) * 4, pi * 128:(pi + 1) * 128]
                    cast_unit(src, dst, 512, 128)
        for gi in range(NG_DN):
            for kq in range(4):
                src = w_dn[l * DFF + kq * 1024:l * DFF + (kq + 1) * 1024, gi * 128:(gi + 1) * 128].rearrange("(k p) c -> p k c", p=128)
                r0 = (l * NG_DN + gi) * 128
                dst = wg_dn[r0:r0 + 128, kq * 1024:(kq + 1) * 1024].rearrange("p (k c) -> p k c", c=128)
                cast_unit(src, dst, 1024, 128)
    P.barrier()

    wctr = [0]
    dbank = [0]

    def load_w(src_rows):
        i = wctr[0] % NWB
        wctr[0] += 1
        P.dma("sp", wbuf[i][:], src_rows, reads=[bf("wscr")], writes=[bf("wbuf%d" % i)])
        return wbuf[i], bf("wbuf%d" % i)

    def next_bank():
        b = dbank[0] % 4
        dbank[0] += 1
        return b

    def mm(out, lhsT, rhs, start, stop, reads, writes):
        P.op("pe", lambda e: e.matmul(out, lhsT=lhsT, rhs=rhs, start=start, stop=stop), reads=reads, writes=writes)

    def rmsnorm_stats(src, nfeat_chunks, src_bufs):
        for q4 in range(nfeat_chunks // 4):
            P.op("act", lambda e, q4=q4: e.activation(out=regA[:, q4 * 4 * T:(q4 + 1) * 4 * T], in_=src[:, q4 * 4 * T:(q4 + 1) * 4 * T], func=AF.Square),
                 reads=src_bufs, writes=RA[q4 * 4:(q4 + 1) * 4])
        for c in range(nfeat_chunks):
            mm(bank(7), ones, regA[:, c * T:(c + 1) * T], c == 0, c == nfeat_chunks - 1, [bf("cbf"), RA[c]], [bbank(7)])
        P.op("act", lambda e: e.activation(out=rstd[:], in_=bank(7), func=AF.Sqrt, bias=EPS, scale=1.0 / (128 * nfeat_chunks)),
             reads=[bbank(7)], writes=[bf("rstd")])
        P.op("dve", lambda e: e.reciprocal(out=rstd[:], in_=rstd[:]), reads=[bf("rstd")], writes=[bf("rstd")])

    def key_blocks(i0):
        out = []
        for j in range(i0 + 3, max(0, i0 - 8) - 1, -1):
            ia = max(j, i0)
            ib = min(j + 8, i0 + 3)
            diag = j - i0 if i0 <= j <= i0 + 3 else None
            edge = j + 8 - i0 if i0 <= j + 8 <= i0 + 3 else None
            out.append((j, (ia - i0) * 128, (ib - i0 + 1) * 128, diag, edge))
        return out

    def ring_pos(j):
        return ((j // 4) % 3) * 4 + (j % 4)

    def ringbuf(name, j):
        return bf("%s_s%d" % (name, (j // 4) % 3))

    def body(l, t):
        i0 = 4 * t
        slot = t % 3
        gc = lambda n, c: col(pp, ppl.g + (l * 4 + n) * 16 + c)
        bxt = bf("xt")
        if PROFILE:
            P.scope = "A_load_norm1_%d_%d" % (l, t)
        xsrc = xs[l].rearrange("(c p) s -> p c s", p=128)[:, :, t * T:(t + 1) * T]
        P.dma("pool", xt[:].rearrange("p (c s) -> p c s", s=T), xsrc,
              reads=[bf("x%d_%d" % (l, t))] if l > 0 else [], writes=[bxt])
        if t == 0:
            for nm, tn in (("lru_state", lru_state), ("lru_halo", lru_halo), ("sc_halo", sc_halo), ("ffn_halo", ffn_halo)):
                P.op("pool", lambda e, tn=tn: e.memset(tn[:], 0.0), writes=[bf(nm)])
        rmsnorm_stats(xt, NCH, [bxt])
        for c in range(NCH):
            P.op("dve", lambda e, c=c: e.scalar_tensor_tensor(out=regA[:, c * T:(c + 1) * T], in0=xt[:, c * T:(c + 1) * T], scalar=gc(0, c),
                                                              in1=rstd[:], op0=ALU.mult, op1=ALU.mult),
                 reads=[bxt, bf("rstd"), bf("pp")], writes=[RA[c]])
        if PROFILE:
            P.scope = "C_win_%d_%d" % (l, t)
        for gi, (kind, idx, cols) in enumerate(WIN_G):
            r0 = (l * NG_IN + gi) * 128
            wb, wbb = load_w(wg_in[r0:r0 + 128, :])
            if kind in ("v_sb", "v_df"):
                ring = vr_sb if kind == "v_sb" else vr_df
                rname = "vr_sb" if kind == "v_sb" else "vr_df"
                for tb in range(4):
                    b = next_bank()
                    for kc in range(NCH):
                        mm(ps[:, b * 512:b * 512 + 256], regA[:, kc * T + tb * 128:kc * T + (tb + 1) * 128], wb[:, kc * 256:(kc + 1) * 256],
                           kc == 0, kc == NCH - 1, [RA[kc], wbb], [bbank(b)])
                    rb = slot * 4 + tb
                    dst = ring[:, rb * 512 + idx * 256:rb * 512 + (idx + 1) * 256]
                    eng = "act" if tb % 2 == 0 else "dve"
                    if eng == "act":
                        P.op("act", lambda e, b=b, dst=dst: e.activation(out=dst, in_=ps[:, b * 512:b * 512 + 256], func=AF.Copy),
                             reads=[bbank(b)], writes=[bf("%s_s%d" % (rname, slot))])
                    else:
                        P.op("dve", lambda e, b=b, dst=dst: e.tensor_copy(out=dst, in_=ps[:, b * 512:b * 512 + 256]),
                             reads=[bbank(b)], writes=[bf("%s_s%d" % (rname, slot))])
                continue
            banks = []
            for ci in range(2):
                b = next_bank()
                banks.append(b)
                for kc in range(NCH):
                    mm(bank(b), wb[:, kc * 256 + ci * 128:kc * 256 + (ci + 1) * 128], regA[:, kc * T:(kc + 1) * T],
                       kc == 0, kc == NCH - 1, [RA[kc], wbb], [bbank(b)])
            if kind in ("q_sb", "q_df", "k_sb", "k_df"):
                for ci in range(2):
                    c = idx * 2 + ci
                    b = banks[ci]
                    if kind == "q_sb":
                        dst, wbuf_ = qT[:, c * T:(c + 1) * T], bf("qT")
                    elif kind == "q_df":
                        dst, wbuf_ = qT[:, (4 + c) * T:(5 + c) * T], bf("qT")
                    elif kind == "k_sb":
                        dst, wbuf_ = kr_sb[:, c * NRB + slot * 512:c * NRB + (slot + 1) * 512], bf("kr_sb_s%d" % slot)
                    else:
                        dst, wbuf_ = kr_df[:, c * NRB + slot * 512:c * NRB + (slot + 1) * 512], bf("kr_df_s%d" % slot)
                    if ci == 0:
                        P.op("act", lambda e, b=b, dst=dst: e.activation(out=dst, in_=bank(b), func=AF.Copy), reads=[bbank(b)], writes=[wbuf_])
                    else:
                        P.op("dve", lambda e, b=b, dst=dst: e.tensor_copy(out=dst, in_=bank(b)), reads=[bbank(b)], writes=[wbuf_])
            elif kind == "lru":
                c = idx
                r = 0
                bx_, bg_ = banks
                lx = lxb[r]
                blx = bf("lxb%d" % r)
                P.op("pool", lambda e, lx=lx, c=c: e.tensor_copy(out=lx[:, 0:3], in_=lru_halo[:, c * 4:c * 4 + 3]), reads=[bf("lru_halo")], writes=[blx])
                P.op("act", lambda e, lx=lx, b=bx_: e.activation(out=lx[:, 3:3 + T], in_=bank(b), func=AF.Copy), reads=[bbank(bx_)], writes=[blx])
                P.op("pool", lambda e, lx=lx, c=c: e.tensor_copy(out=lru_halo[:, c * 4:c * 4 + 3], in_=lx[:, T:T + 3]), reads=[blx], writes=[bf("lru_halo")])
                xc = tA[r]
                bxc = bf("tA%d" % r)
                wk = lambda k: col(pp, ppl.lcw + (l * 4 + k) * 4 + c)
                P.op("dve", lambda e, lx=lx, xc=xc: e.tensor_scalar(out=xc[:], in0=lx[:, 3:3 + T], scalar1=wk(3), scalar2=col(pp, ppl.lcb + l * 4 + c),
                                                                    op0=ALU.mult, op1=ALU.add), reads=[blx, bf("pp")], writes=[bxc])
                for k in range(3):
                    P.op("dve", lambda e, lx=lx, xc=xc, k=k: e.scalar_tensor_tensor(out=xc[:], in0=lx[:, k:k + T], scalar=wk(k), in1=xc[:],
                                                                                    op0=ALU.mult, op1=ALU.add), reads=[blx, bxc, bf("pp")], writes=[bxc])
                P.op("pool", lambda e, xc=xc, r=r: e.tensor_copy(out=xcb[r][:], in_=xc[:]), reads=[bxc], writes=[bf("xcb%d" % r)])
                gb = []
                for g in range(2):
                    b = next_bank()
                    gb.append(b)
                    off = ((l * 2 + g) * 4 + c) * 128
                    mm(bank(b), bdgb[:, off:off + 128], xcb[r][:], True, True, [bf("bdgb"), bf("xcb%d" % r)], [bbank(b)])
                rr, ii = tB[r], tC[r]
                P.op("act", lambda e, rr=rr, b=gb[0]: e.activation(out=rr[:], in_=bank(b), func=AF.Sigmoid, bias=col(pp, ppl.bg + (l * 2 + 0) * 4 + c)),
                     reads=[bbank(gb[0]), bf("pp")], writes=[bf("tB%d" % r)])
                P.op("act", lambda e, ii=ii, b=gb[1]: e.activation(out=ii[:], in_=bank(b), func=AF.Sigmoid, bias=col(pp, ppl.bg + (l * 2 + 1) * 4 + c)),
                     reads=[bbank(gb[1]), bf("pp")], writes=[bf("tC%d" % r)])
                P.op("dve", lambda e, ii=ii, xc=xc: e.tensor_tensor(out=ii[:], in0=ii[:], in1=xc[:], op=ALU.mult), reads=[bf("tC%d" % r), bxc], writes=[bf("tC%d" % r)])
                a2 = tD[r]
                P.op("act", lambda e, a2=a2, rr=rr: e.activation(out=a2[:], in_=rr[:], func=AF.Exp, scale=col(sp16, l * 4 + c)),
                     reads=[bf("tB%d" % r), bf("sp8")], writes=[bf("tD%d" % r)])
                P.op("act", lambda e, a2=a2: e.activation(out=a2[:], in_=a2[:], func=AF.Sqrt, bias=1.0, scale=-1.0), reads=[bf("tD%d" % r)], writes=[bf("tD%d" % r)])
                P.op("act", lambda e, rr=rr: e.activation(out=rr[:], in_=rr[:], func=AF.Exp, scale=col(sp8, l * 4 + c)),
                     reads=[bf("tB%d" % r), bf("sp8")], writes=[bf("tB%d" % r)])
                P.op("dve", lambda e, ii=ii, a2=a2: e.tensor_tensor(out=ii[:], in0=ii[:], in1=a2[:], op=ALU.mult), reads=[bf("tC%d" % r), bf("tD%d" % r)], writes=[bf("tC%d" % r)])
                P.op("dve", lambda e, xc=xc, rr=rr, ii=ii: e.tensor_tensor_scan(out=xc[:], data0=rr[:], data1=ii[:], initial=lru_state[:, c:c + 1],
                                                                                op0=ALU.mult, op1=ALU.add),
                     reads=[bf("tB%d" % r), bf("tC%d" % r), bf("lru_state")], writes=[bxc])
                P.op("dve", lambda e, xc=xc: e.tensor_copy(out=lru_state[:, c:c + 1], in_=xc[:, T - 1:T]), reads=[bxc], writes=[bf("lru_state")])
                P.op("act", lambda e, a2=a2, b=bg_: e.activation(out=a2[:], in_=bank(b), func=AF.Gelu_apprx_tanh), reads=[bbank(bg_)], writes=[bf("tD%d" % r)])
                P.op("dve", lambda e, xc=xc, a2=a2: e.tensor_tensor(out=ycs[:, c * T:(c + 1) * T], in0=xc[:], in1=a2[:], op=ALU.mult),
                     reads=[bxc, bf("tD%d" % r)], writes=[bf("ycs")])
            elif kind == "sc_cx":
                c = idx
                r = 0
                bc_, bx_ = banks
                tmp = tA[r]
                mb = lxb[r]
                bmb = bf("lxb%d" % r)
                P.op("act", lambda e, tmp=tmp, b=bc_: e.activation(out=tmp[:], in_=bank(b), func=AF.Copy), reads=[bbank(bc_)], writes=[bf("tA%d" % r)])
                P.op("pool", lambda e, mb=mb: e.tensor_copy(out=mb[:, 0:2], in_=sc_halo[:, c * 2:c * 2 + 2]), reads=[bf("sc_halo")], writes=[bmb])
                P.op("dve", lambda e, mb=mb, tmp=tmp, b=bx_: e.tensor_tensor(out=mb[:, 2:2 + T], in0=tmp[:], in1=bank(b), op=ALU.mult),
                     reads=[bf("tA%d" % r), bbank(bx_)], writes=[bmb])
                P.op("pool", lambda e, mb=mb: e.tensor_copy(out=sc_halo[:, c * 2:c * 2 + 2], in_=mb[:, T:T + 2]), reads=[bmb], writes=[bf("sc_halo")])
                wk = lambda k: col(pp, ppl.scw + (l * 3 + k) * 4 + c)
                cm = convm[:, c * T:(c + 1) * T]
                P.op("dve", lambda e, mb=mb, cm=cm: e.tensor_scalar(out=cm, in0=mb[:, 2:2 + T], scalar1=wk(2), scalar2=None, op0=ALU.mult),
                     reads=[bmb, bf("pp")], writes=[bf("convm")])
                for k in range(2):
                    P.op("dve", lambda e, mb=mb, cm=cm, k=k: e.scalar_tensor_tensor(out=cm, in0=mb[:, k:k + T], scalar=wk(k), in1=cm, op0=ALU.mult, op1=ALU.add),
                         reads=[bmb, bf("convm"), bf("pp")], writes=[bf("convm")])
            elif kind == "sc_b":
                for ci in range(2):
                    c = idx * 2 + ci
                    b = banks[ci]
                    P.op("dve", lambda e, b=b, c=c: e.tensor_tensor(out=ycs[:, (4 + c) * T:(5 + c) * T], in0=convm[:, c * T:(c + 1) * T], in1=bank(b), op=ALU.mult),
                         reads=[bf("convm"), bbank(b)], writes=[bf("ycs")])

        if PROFILE:
            P.scope = "D_sb_%d_%d" % (l, t)
        kbl = key_blocks(i0)
        nb = len(kbl)
        zq = [0]

        def znext():
            zq[0] += 1
            return (0, 1, 5, 6)[zq[0] % 4]

        for c in range(4):
            for X in range(2):
                P.op("pool", lambda e, X=X: e.memset(cs32[X][:], 0.0), writes=[bf("cs32_%d" % X)])
                pbx = X * 64
                mm(ps[pbx:pbx + 64, 4 * 512:5 * 512], ones[:, 0:64], zeros_bf[:], True, False, [bf("zeros"), bf("cbf")], [bf("o_sb%d" % X), bbank(4)])
            zbank = {}

            def stage1a(X, n):
                j, ca, cb, diag, edge = kbl[n]
                pb = X * 64
                r = n % 2
                zb = znext()
                kcol = c * NRB + ring_pos(j) * 128
                mm(ps[:, zb * 512 + ca:zb * 512 + cb], kr_sb[pb:pb + 64, kcol:kcol + 128], qT[pb:pb + 64, c * T + ca:c * T + cb], True, True,
                   [ringbuf("kr_sb", j), bf("qT")], [bbank(zb)])
                P.op("act", lambda e: e.activation(out=Et[X][r][:, ca:cb], in_=ps[:, zb * 512 + ca:zb * 512 + cb], func=AF.Exp, scale=0.125),
                     reads=[bbank(zb)], writes=[bf("Et%d_%d" % (X, r))])

            def stage1b(X, n):
                j, ca, cb, diag, edge = kbl[n]
                r = n % 2
                P.op("act", lambda e: e.activation(out=Lp[X][r][:, ca:cb], in_=Et[X][r][:, ca:cb], func=AF.Ln, bias=1.0),
                     reads=[bf("Et%d_%d" % (X, r))], writes=[bf("Lp%d_%d" % (X, r))])
                for sub, m in ((diag, mdiag), (edge, medge)):
                    if sub is not None:
                        P.op("dve", lambda e, sub=sub, m=m: e.tensor_tensor(out=Lp[X][r][:, sub * 128:(sub + 1) * 128], in0=Lp[X][r][:, sub * 128:(sub + 1) * 128], in1=m, op=ALU.mult),
                             reads=[bf("Lp%d_%d" % (X, r)), bf("cbf")], writes=[bf("Lp%d_%d" % (X, r))])

            def stage2a(X, n):
                j, ca, cb, diag, edge = kbl[n]
                r = n % 2
                sbk = 2 + X
                mm(ps[:, sbk * 512 + ca:sbk * 512 + cb], negtri_i, Lp[X][r][:, ca:cb], True, n == 0, [bf("cbf"), bf("Lp%d_%d" % (X, r))], [bbank(sbk)])
                if n > 0:
                    mm(ps[:, sbk * 512 + ca:sbk * 512 + cb], nof32[:], cs32[X][:, ca:cb], False, True, [bf("nof32"), bf("cs32_%d" % X)], [bbank(sbk)])
                P.op("act", lambda e: e.activation(out=tt[X][:, ca:cb], in_=ps[:, sbk * 512 + ca:sbk * 512 + cb], func=AF.Exp),
                     reads=[bbank(sbk)], writes=[bf("tt%d" % X)])
                if n < nb - 1:
                    P.op("pool", lambda e: e.tensor_tensor(out=cs32[X][:, ca:cb], in0=cs32[X][:, ca:cb], in1=Lp[X][r][:, ca:cb], op=ALU.add),
                         reads=[bf("cs32_%d" % X), bf("Lp%d_%d" % (X, r))], writes=[bf("cs32_%d" % X)])

            def stage2b(X, n):
                j, ca, cb, diag, edge = kbl[n]
                r = n % 2
                P.op("dve", lambda e: e.tensor_tensor(out=wt[X][r][:, ca:cb], in0=Et[X][r][:, ca:cb], in1=tt[X][:, ca:cb], op=ALU.mult),
                     reads=[bf("Et%d_%d" % (X, r)), bf("tt%d" % X)], writes=[bf("wt%d_%d" % (X, r))])
                for sub, m in ((diag, mdiag), (edge, medge)):
                    if sub is not None:
                        P.op("dve", lambda e, sub=sub, m=m: e.tensor_tensor(out=wt[X][r][:, sub * 128:(sub + 1) * 128], in0=wt[X][r][:, sub * 128:(sub + 1) * 128], in1=m, op=ALU.mult),
                             reads=[bf("wt%d_%d" % (X, r)), bf("cbf")], writes=[bf("wt%d_%d" % (X, r))])

            def stage3(X, n):
                j, ca, cb, diag, edge = kbl[n]
                r = n % 2
                pb = X * 64
                hd = 2 * c + X
                vcol = ring_pos(j) * 512 + hd * 64
                mm(ps[pb:pb + 64, 4 * 512 + ca:4 * 512 + cb], vr_sb[:, vcol:vcol + 64], wt[X][r][:, ca:cb], False, n == nb - 1,
                   [ringbuf("vr_sb", j), bf("wt%d_%d" % (X, r))], [bf("o_sb%d" % X)])

            for n in range(nb + 3):
                if 2 <= n <= nb + 1:
                    for X in range(2):
                        stage2b(X, n - 2)
                if n >= 3:
                    for X in range(2):
                        stage3(X, n - 3)
                if n < nb:
                    for X in range(2):
                        stage1a(X, n)
                    for X in range(2):
                        stage1b(X, n)
                if 1 <= n <= nb:
                    for X in range(2):
                        stage2a(X, n - 1)
            P.op("act", lambda e, c=c: e.activation(out=regA[:, c * T:(c + 1) * T], in_=bank(4), func=AF.Copy),
                 reads=[bf("o_sb0"), bf("o_sb1"), bbank(4)], writes=[RA[c]])
        if PROFILE:
            P.scope = "D_df_%d_%d" % (l, t)
        for h in range(4):
            obk = (2, 3)
            dbk = (4, 7)
            for X in range(2):
                mm(bank(obk[X]), ones, zeros_bf[:], True, False, [bf("zeros"), bf("cbf")], [bbank(obk[X])])
                mm(bank(dbk[X]), ones, zeros_bf[:], True, False, [bf("zeros"), bf("cbf")], [bbank(dbk[X])])
            zbank = {}

            def dstage1(X, n):
                j, ca, cb, diag, edge = kbl[n]
                pb = X * 64
                r = n % 2
                zb = znext()
                kcol = h * NRB + ring_pos(j) * 128
                mm(ps[:, zb * 512 + ca:zb * 512 + cb], kr_df[pb:pb + 64, kcol:kcol + 128], qT[pb:pb + 64, (4 + h) * T + ca:(4 + h) * T + cb], True, True,
                   [ringbuf("kr_df", j), bf("qT")], [bbank(zb)])
                near = []
                for sub in range(ca // 128, cb // 128):
                    d = i0 + sub - j
                    if d in (0, 1):
                        near.append((sub, d))
                fa, fb = ca, cb
                for sub, d in near:
                    s0, s1 = sub * 128, (sub + 1) * 128
                    P.op("dve", lambda e, s0=s0, s1=s1, d=d: e.scalar_tensor_tensor(out=tt[X][:, s0:s1], in0=ps[:, zb * 512 + s0:zb * 512 + s1], scalar=0.125,
                                                                                     in1=btab[:, h * 256 + d * 128:h * 256 + (d + 1) * 128], op0=ALU.mult, op1=ALU.add),
                         reads=[bbank(zb), bf("btab")], writes=[bf("tt%d" % X)])
                    P.op("act", lambda e, s0=s0, s1=s1: e.activation(out=wt[X][r][:, s0:s1], in_=tt[X][:, s0:s1], func=AF.Exp),
                         reads=[bf("tt%d" % X)], writes=[bf("wt%d_%d" % (X, r))])
                    if s0 == fa:
                        fa = s1
                    elif s1 == fb:
                        fb = s0
                if fb > fa:
                    P.op("act", lambda e, fa=fa, fb=fb: e.activation(out=wt[X][r][:, fa:fb], in_=ps[:, zb * 512 + fa:zb * 512 + fb], func=AF.Exp, scale=0.125,
                                                                    bias=col(pp, ppl.bfar + h)),
                         reads=[bbank(zb), bf("pp")], writes=[bf("wt%d_%d" % (X, r))])
                if edge is not None:
                    P.op("pool", lambda e, edge=edge: e.memset(wt[X][r][0:64, edge * 128 + 64:edge * 128 + 128], 0.0), writes=[bf("wt%d_%d" % (X, r))])

            def dstage2(X, n):
                j, ca, cb, diag, edge = kbl[n]
                r = n % 2
                vcol = ring_pos(j) * 512 + h * 128
                mm(ps[:, obk[X] * 512 + ca:obk[X] * 512 + cb], vr_df[:, vcol:vcol + 128], wt[X][r][:, ca:cb], False, n == nb - 1,
                   [ringbuf("vr_df", j), bf("wt%d_%d" % (X, r))], [bbank(obk[X])])
                mm(ps[:, dbk[X] * 512 + ca:dbk[X] * 512 + cb], ones, wt[X][r][:, ca:cb], False, n == nb - 1,
                   [bf("cbf"), bf("wt%d_%d" % (X, r))], [bbank(dbk[X])])

            if DBG_SEQ:
                for X in range(2):
                    for n in range(nb + 1):
                        if n < nb:
                            dstage1(X, n)
                        if n >= 1:
                            dstage2(X, n - 1)
            else:
                for n in range(nb + 1):
                    for X in range(2):
                        if n < nb:
                            dstage1(X, n)
                    for X in range(2):
                        if n >= 1:
                            dstage2(X, n - 1)
            a0, a1, r0_, r1_ = tA[0], tB[0], tC[0], tD[0]
            P.op("dve", lambda e: e.reciprocal(out=r0_[:], in_=bank(4)), reads=[bbank(4)], writes=[bf("tC0")])
            P.op("dve", lambda e: e.tensor_tensor(out=a0[:], in0=r0_[:], in1=bank(2), op=ALU.mult), reads=[bf("tC0"), bbank(2)], writes=[bf("tA0")])
            P.op("dve", lambda e: e.reciprocal(out=r1_[:], in_=bank(7)), reads=[bbank(7)], writes=[bf("tD0")])
            P.op("dve", lambda e: e.tensor_tensor(out=a1[:], in0=r1_[:], in1=bank(3), op=ALU.mult), reads=[bf("tD0"), bbank(3)], writes=[bf("tB0")])
            P.op("dve", lambda e: e.scalar_tensor_tensor(out=a0[:], in0=a1[:], scalar=col(neglam, l), in1=a0[:], op0=ALU.mult, op1=ALU.add),
                 reads=[bf("tA0"), bf("tB0"), bf("neglam")], writes=[bf("tA0")])
            P.op("act", lambda e: e.activation(out=xcb[0][:], in_=a0[:], func=AF.Square), reads=[bf("tA0")], writes=[bf("xcb0")])
            mm(bank(7), ones, xcb[0][:], True, True, [bf("cbf"), bf("xcb0")], [bbank(7)])
            P.op("act", lambda e: e.activation(out=r0_[:], in_=bank(7), func=AF.Sqrt, bias=EPS, scale=1.0 / 128), reads=[bbank(7)], writes=[bf("tC0")])
            P.op("dve", lambda e: e.reciprocal(out=r0_[:], in_=r0_[:]), reads=[bf("tC0")], writes=[bf("tC0")])
            P.op("dve", lambda e, h=h: e.scalar_tensor_tensor(out=regA[:, (4 + h) * T:(5 + h) * T], in0=a0[:], scalar=col(gsl, l), in1=r0_[:], op0=ALU.mult, op1=ALU.mult),
                 reads=[bf("tA0"), bf("tC0"), bf("gsl")], writes=[RA[4 + h]])
        if PROFILE:
            P.scope = "E_wout_%d_%d" % (l, t)
        P.fence([bf(n) for n in ("qT", "convm", "tA0", "tB0", "tC0", "tD0", "ubg0", "ubu0")])
        for gi in range(NG_OUT):
            r0 = (l * NG_OUT + gi) * 128
            wb, wbb = load_w(wg_out[r0:r0 + 128, :])
            for ci in range(2):
                b = next_bank()
                mc = gi * 2 + ci
                for kc in range(NCH):
                    rhs = regA[:, kc * T:(kc + 1) * T] if kc < 8 else ycs[:, (kc - 8) * T:(kc - 7) * T]
                    mm(bank(b), wb[:, kc * 256 + ci * 128:kc * 256 + (ci + 1) * 128], rhs, kc == 0, kc == NCH - 1,
                       [RA[kc] if kc < 8 else bf("ycs"), wbb], [bbank(b)])
                if ci == 0:
                    P.op("act", lambda e, b=b, mc=mc: e.activation(out=mixed[:, mc * T:(mc + 1) * T], in_=bank(b), func=AF.Copy), reads=[bbank(b)], writes=[bf("mixed")])
                else:
                    P.op("dve", lambda e, b=b, mc=mc: e.tensor_copy(out=mixed[:, mc * T:(mc + 1) * T], in_=bank(b)), reads=[bbank(b)], writes=[bf("mixed")])

        def post_norm_residual(nidx):
            rmsnorm_stats(mixed, NCH, [bf("mixed")])
            for c in range(NCH):
                P.op("dve", lambda e, c=c: e.scalar_tensor_tensor(out=mixed[:, c * T:(c + 1) * T], in0=mixed[:, c * T:(c + 1) * T], scalar=gc(nidx, c),
                                                                  in1=rstd[:], op0=ALU.mult, op1=ALU.mult),
                     reads=[bf("mixed"), bf("rstd"), bf("pp")], writes=[bf("mixed")])
            P.op("dve", lambda e: e.tensor_tensor(out=xt[:], in0=xt[:], in1=mixed[:], op=ALU.add), reads=[bxt, bf("mixed")], writes=[bxt])

        post_norm_residual(1)
        if PROFILE:
            P.scope = "F_norm2_%d_%d" % (l, t)
        P.fence([bf("mixed")])
        P.fence([bf(n) for n in ("ycs", "Et0_0", "Et0_1", "Et1_0", "Et1_1", "tt0", "tt1", "cs32_0", "cs32_1", "Lp0_0", "Lp0_1", "Lp1_0", "Lp1_1", "wt0_0", "wt0_1", "wt1_0", "wt1_1", "xcb0", "lxb0")])
        rmsnorm_stats(xt, NCH, [bxt])
        for c in range(NCH):
            P.op("dve", lambda e, c=c: e.scalar_tensor_tensor(out=regA[:, c * T:(c + 1) * T], in0=xt[:, c * T:(c + 1) * T], scalar=gc(2, c),
                                                              in1=rstd[:], op0=ALU.mult, op1=ALU.mult),
                 reads=[bxt, bf("rstd"), bf("pp")], writes=[RA[c]])
        if PROFILE:
            P.scope = "F_wup_%d_%d" % (l, t)
        for gi in range(NG_UP):
            r0 = (l * NG_UP + gi) * 128
            wb, wbb = load_w(wg_up[r0:r0 + 128, :])
            r = 0
            bks = []
            for ci in range(2):
                b = next_bank()
                bks.append(b)
                for kc in range(NCH):
                    mm(bank(b), wb[:, kc * 256 + ci * 128:kc * 256 + (ci + 1) * 128], regA[:, kc * T:(kc + 1) * T], kc == 0, kc == NCH - 1,
                       [RA[kc], wbb], [bbank(b)])
            outs = []
            for ci, (ub, nm, ch) in enumerate(((ubg[r], "ubg%d" % r, gi), (ubu[r], "ubu%d" % r, 32 + gi))):
                b = bks[ci]
                P.op("pool", lambda e, ub=ub, ch=ch: e.tensor_copy(out=ub[:, 0:2], in_=ffn_halo[:, ch * 2:ch * 2 + 2]), reads=[bf("ffn_halo")], writes=[bf(nm)])
                P.op("act", lambda e, ub=ub, b=b: e.activation(out=ub[:, 2:2 + T], in_=bank(b), func=AF.Copy), reads=[bbank(b)], writes=[bf(nm)])
                P.op("pool", lambda e, ub=ub, ch=ch: e.tensor_copy(out=ffn_halo[:, ch * 2:ch * 2 + 2], in_=ub[:, T:T + 2]), reads=[bf(nm)], writes=[bf("ffn_halo")])
                cv = tC[r] if ci == 0 else tD[r]
                cvn = ("tC%d" if ci == 0 else "tD%d") % r
                wk = lambda k, ch=ch: col(pp, ppl.fcw + (l * 3 + k) * 64 + ch)
                P.op("dve", lambda e, ub=ub, cv=cv, wk=wk: e.tensor_scalar(out=cv[:], in0=ub[:, 2:2 + T], scalar1=wk(2), scalar2=None, op0=ALU.mult),
                     reads=[bf(nm), bf("pp")], writes=[bf(cvn)])
                for k in range(2):
                    P.op("dve", lambda e, ub=ub, cv=cv, wk=wk, k=k: e.scalar_tensor_tensor(out=cv[:], in0=ub[:, k:k + T], scalar=wk(k), in1=cv[:], op0=ALU.mult, op1=ALU.add),
                         reads=[bf(nm), bf(cvn), bf("pp")], writes=[bf(cvn)])
                outs.append((cv, cvn))
            (cg, cgn), (cu, cun) = outs
            P.op("act", lambda e, cg=cg: e.activation(out=cg[:], in_=cg[:], func=AF.Gelu_apprx_tanh), reads=[bf(cgn)], writes=[bf(cgn)])
            P.op("dve", lambda e, cg=cg, cu=cu, gi=gi: e.tensor_tensor(out=act[:, gi * T:(gi + 1) * T], in0=cg[:], in1=cu[:], op=ALU.mult),
                 reads=[bf(cgn), bf(cun)], writes=[bf("act")])
        P.fence([bf(n) for n in ("qT", "convm", "tA0", "tB0", "tC0", "tD0", "ubg0", "ubu0")])
        if PROFILE:
            P.scope = "F_wdn_%d_%d" % (l, t)
        for gi in range(NG_DN):
            r0 = (l * NG_DN + gi) * 128
            wb, wbb = load_w(wg_dn[r0:r0 + 128, :])
            b = next_bank()
            for kc in range(32):
                mm(bank(b), wb[:, kc * 128:(kc + 1) * 128], act[:, kc * T:(kc + 1) * T], kc == 0, kc == 31, [bf("act"), wbb], [bbank(b)])
            if gi % 2 == 0:
                P.op("act", lambda e, b=b, gi=gi: e.activation(out=mixed[:, gi * T:(gi + 1) * T], in_=bank(b), func=AF.Copy), reads=[bbank(b)], writes=[bf("mixed")])
            else:
                P.op("dve", lambda e, b=b, gi=gi: e.tensor_copy(out=mixed[:, gi * T:(gi + 1) * T], in_=bank(b)), reads=[bbank(b)], writes=[bf("mixed")])
        if PROFILE:
            P.scope = "G_norm_store_%d_%d" % (l, t)
        post_norm_residual(3)
        P.fence([bf("mixed"), bf("act")])
        xdst = xs[l + 1].rearrange("(c p) s -> p c s", p=128)[:, :, t * T:(t + 1) * T]
        P.dma("pool", xdst, xt[:].rearrange("p (c s) -> p c s", s=T), reads=[bxt], writes=[bf("x%d_%d" % (l + 1, t))], final=(l == L - 1))

    for l in range(L):
        for t in range(NT):
            body(l, t)
    P.emit()
    st.close()
    return nc


def _t5_bucket_np(rel):
    nb = 16
    max_exact = 8
    ret = np.where(rel > 0, nb, 0)
    n = np.abs(rel)
    nf = np.maximum(n, 1).astype(np.float32) / np.float32(max_exact)
    large = max_exact + (np.log(nf).astype(np.float32) / np.float32(math.log(128 / max_exact)) * np.float32(nb - max_exact)).astype(np.int32)
    large = np.minimum(large, nb - 1)
    return ret + np.where(n < max_exact, n, large)


def _prep_shared(inp, L):
    f = lambda a: np.ascontiguousarray(np.asarray(a, dtype=np.float32))
    ppl = PPL(L)
    pp = np.zeros((128, ppl.n), np.float32)
    cm = lambda a: np.asarray(a, np.float32).reshape(-1, 128).T
    pp[:, ppl.g:ppl.g + L * 64] = cm(inp["norm_gains"])
    pp[:, ppl.lcw:ppl.lcw + L * 16] = cm(inp["lru_conv_w"])
    pp[:, ppl.lcb:ppl.lcb + L * 4] = cm(inp["lru_conv_b"])
    pp[:, ppl.bg:ppl.bg + L * 8] = cm(inp["lru_b_gate"])
    pp[:, ppl.ll:ppl.ll + L * 4] = cm(inp["lru_lambda"])
    pp[:, ppl.scw:ppl.scw + L * 12] = cm(inp["sc_conv_w"])
    pp[:, ppl.fcw:ppl.fcw + L * 192] = cm(inp["ffn_conv_w"])
    pp[:, ppl.sub:ppl.sub + L] = cm(inp["diff_subln_g"])
    rb = np.asarray(inp["rel_bias"], np.float32)
    pp[:, ppl.bfar:ppl.bfar + 4] = np.broadcast_to(rb[15][None, :], (128, 4))
    lamraw = np.ascontiguousarray(np.broadcast_to(np.asarray(inp["diff_lambda"], np.float32).reshape(1, L * 256), (128, L * 256)))
    wg = np.asarray(inp["lru_w_gate"], np.float32)
    bdg = np.zeros((128, L * 1024), np.float32)
    for l in range(L):
        for g in range(2):
            for c in range(4):
                off = ((l * 2 + g) * 4 + c) * 128
                bdg[0:64, off:off + 64] = wg[l, g, 2 * c]
                bdg[64:128, off + 64:off + 128] = wg[l, g, 2 * c + 1]
    s_ = np.arange(128)[:, None]
    q_ = np.arange(128)[None, :]
    consts = np.zeros((128, 896), np.float32)
    consts[:, 0:128] = -(s_ > q_).astype(np.float32)
    consts[:, 128:256] = -1.0
    consts[:, 256:384] = 1.0
    consts[:, 384:512] = (s_ < q_).astype(np.float32)
    consts[:, 512:640] = (s_ >= q_).astype(np.float32)
    consts[:, 640:768] = -(s_ >= q_).astype(np.float32)
    consts[:, 768:896] = np.where((s_ >= 64) & (q_ < 64), NEG, 0.0)
    biasT = np.zeros((128, 1024), np.float32)
    for d in range(2):
        bk = _t5_bucket_np(s_ - q_ - 128 * d)
        for h in range(4):
            biasT[:, h * 256 + d * 128:h * 256 + (d + 1) * 128] = rb[bk, h]
    return dict(
        w_in=f(inp["w_in"]).reshape(L * D, INC), w_out=f(inp["w_out"]).reshape(L * D, D),
        w_up=f(inp["ffn_w_up"]).reshape(L * D, 2 * DFF), w_dn=f(inp["ffn_w_down"]).reshape(L * DFF, D),
        pp=pp, lamraw=lamraw, bdg=bdg, consts=consts, biasT=biasT)


def run(inputs, L=None):
    x = np.asarray(inputs["x"], np.float32)
    Bn, S, _ = x.shape
    if L is None:
        L = int(np.asarray(inputs["w_in"]).shape[0])
    shared = _prep_shared(inputs, L)
    nc = build(S, L)
    in_maps = []
    for b in range(Bn):
        m = dict(shared)
        m["xT"] = np.ascontiguousarray(x[b].T)
        in_maps.append(m)
    res = run_bass_kernel_spmd(nc, in_maps, core_ids=list(range(Bn)))
    out = np.stack([np.ascontiguousarray(res.results[b]["yT"].T) for b in range(Bn)], axis=0)
    return out.astype(np.float32)


def kernel(**inputs):
    return run(inputs)
```

```python
import math
import types
import contextlib
import numpy as np
import concourse.bass as bass
import concourse.mybir as mybir
from concourse.bass_utils import run_bass_kernel_spmd

F32 = mybir.dt.float32
BF16 = mybir.dt.bfloat16
AF = mybir.ActivationFunctionType
ALU = mybir.AluOpType

D = 2048
NCH = 16
T = 512
GW = 512
DFF = 4096
INC = 5632
EPS = 1e-6
NEG = -30000.0
PROFILE = False
DBG_NOFP32 = False
DBG_SEQ = False


def _freeze(fn, depth=0):
    if not isinstance(fn, types.FunctionType) or fn.__closure__ is None or depth > 3:
        return fn
    cells = []
    for c in fn.__closure__:
        try:
            v = c.cell_contents
        except ValueError:
            cells.append(c)
            continue
        if isinstance(v, types.FunctionType) and v.__name__ == "<lambda>":
            v = _freeze(v, depth + 1)
        cells.append(types.CellType(v))
    return types.FunctionType(fn.__code__, fn.__globals__, fn.__name__, fn.__defaults__, tuple(cells))


class Buf:
    __slots__ = ("name", "lw", "rd", "kids")

    def __init__(self, name, kids=None):
        self.name = name
        self.lw = None
        self.rd = {}
        self.kids = kids


def _expand(bufs):
    out = []
    for b in bufs:
        if b.kids:
            out.extend(b.kids)
        else:
            out.append(b)
    return out


class Prog:
    ENGS = ("pe", "dve", "act", "pool", "sp")
    NSLOT = 8

    def __init__(self, nc):
        self.nc = nc
        self.ops = {e: [] for e in self.ENGS}
        self.stack = contextlib.ExitStack()
        self.sem = {e: self.stack.enter_context(nc.semaphore("s_" + e)) for e in self.ENGS}
        self.cnt = {e: 0 for e in self.ENGS}
        self.waited = {e: {} for e in self.ENGS}
        self.signals = {e: set() for e in self.ENGS}
        self.pending = {e: [] for e in self.ENGS}
        self.dq = {}
        for q in ("sp", "pool"):
            self.dq[q] = dict(n=0, sems=[self.stack.enter_context(nc.semaphore("d_%s%d" % (q, i)))
                                         for i in range(self.NSLOT)])
        self.final = {e: [] for e in self.ENGS}
        self.scope = None
        self.opscope = {e: [] for e in self.ENGS}

    def _collect(self, eng, reads, writes):
        reads = _expand(reads)
        writes = _expand(writes)
        deps = []
        for b in reads:
            if b.lw is not None:
                deps.append(b.lw)
        for b in writes:
            if b.lw is not None and not (b.lw[0] == "e" and b.lw[1] == eng):
                deps.append(b.lw)
            for tk in b.rd.values():
                if not (tk[0] == "e" and tk[1] == eng):
                    deps.append(tk)
        if eng == "pe":
            deps = [d for d in deps if not (d[0] == "e" and d[1] == "pe")]
        deps.extend(self.pending[eng])
        self.pending[eng] = []
        need = {}
        for tk in deps:
            k = (tk[0], tk[1])
            if k not in need or need[k] < tk[2]:
                need[k] = tk[2]
        out = []
        w = self.waited[eng]
        for k, v in need.items():
            if w.get(k, 0) < v:
                w[k] = v
                out.append((k[0], k[1], v))
                if k[0] == "e":
                    self.signals[k[1]].add(v)
        return out

    def _update(self, tok, rkey, reads, writes):
        reads = _expand(reads)
        writes = _expand(writes)
        for b in reads:
            b.rd[rkey] = tok
        for b in writes:
            b.lw = tok
            b.rd = {}

    def op(self, eng, fn, reads=(), writes=()):
        waits = self._collect(eng, reads, writes)
        self.cnt[eng] += 1
        tok = ("e", eng, self.cnt[eng])
        self.ops[eng].append((waits, _freeze(fn), ("e", self.cnt[eng])))
        self.opscope[eng].append(self.scope)
        self._update(tok, ("e", eng), reads, writes)
        return tok

    def dma(self, q, out, in_, reads=(), writes=(), final=False):
        d = self.dq[q]
        i = d["n"]
        d["n"] += 1
        slot = i % self.NSLOT
        val = 16 * (i // self.NSLOT + 1)
        waits = self._collect(q, reads, writes)
        key = (q, slot)
        if i >= self.NSLOT:
            w = self.waited[q]
            if w.get(("d", key), 0) < val - 16:
                w[("d", key)] = val - 16
                waits.append(("d", key, val - 16))
        tok = ("d", key, val)
        self.ops[q].append((waits, (lambda e, o=out, i_=in_: e.dma_start(out=o, in_=i_)), ("d", slot)))
        self.opscope[q].append(self.scope)
        self._update(tok, ("d", key), reads, writes)
        if final:
            self.final[q].append(tok)
        return tok

    def fence(self, bufs):
        toks = []
        for b in _expand(bufs):
            if b.lw is not None:
                toks.append(b.lw)
            toks.extend(b.rd.values())
        for e in ("pe", "dve", "act", "pool"):
            self.pending[e].extend([tk for tk in toks if not (tk[0] == "e" and tk[1] == e)])

    def barrier(self):
        toks = [("e", e, self.cnt[e]) for e in self.ENGS if self.cnt[e] > 0]
        for q, d in self.dq.items():
            n = d["n"]
            for slot in range(min(n, self.NSLOT)):
                last_i = ((n - 1 - slot) // self.NSLOT) * self.NSLOT + slot
                toks.append(("d", (q, slot), 16 * (last_i // self.NSLOT + 1)))
        for e in self.ENGS:
            self.pending[e].extend(toks)

    def emit(self):
        nc = self.nc
        rank = {}
        for e in self.ENGS:
            rank[e] = {idx: r + 1 for r, idx in enumerate(sorted(self.signals[e]))}

        def replay(name, e):
            own = self.sem[name]
            sig = self.signals[name]
            cur = None
            for oi, (waits, fn, info) in enumerate(self.ops[name]):
                sc = self.opscope[name][oi]
                if sc != cur:
                    if cur is not None:
                        nc.leave_named_scope(cur, sid, False)
                    if sc is not None:
                        sid, _ = nc.enter_named_scope(sc, False)
                    cur = sc
                for (kind, key, v) in waits:
                    if kind == "e":
                        e.wait_ge(self.sem[key], rank[key][v])
                    else:
                        e.wait_ge(self.dq[key[0]]["sems"][key[1]], v)
                ins = fn(e)
                if info[0] == "e":
                    if info[1] in sig:
                        ins.then_inc(own, 1)
                else:
                    ins.then_inc(self.dq[name]["sems"][info[1]], 16)
            if cur is not None:
                nc.leave_named_scope(cur, sid, False)
            for (kind, key, v) in self.final[name]:
                e.wait_ge(self.dq[key[0]]["sems"][key[1]], v)

        with nc.Block() as block:
            @block.tensor
            def _(e):
                replay("pe", e)

            @block.vector
            def _(e):
                replay("dve", e)

            @block.scalar
            def _(e):
                replay("act", e)

            @block.gpsimd
            def _(e):
                replay("pool", e)

            @block.sync
            def _(e):
                replay("sp", e)
        self.stack.close()


def win_groups():
    g = []
    for name, base in (("q_sb", 0), ("k_sb", 512)):
        for c2 in range(2):
            g.append((name, c2, [base + 256 * c2, base + 256 * c2 + 128]))
    for c2 in range(2):
        g.append(("v_sb", c2, [1024 + 256 * c2, 1024 + 256 * c2 + 128]))
    for name, base in (("q_df", 1536), ("k_df", 2048)):
        for c2 in range(2):
            g.append((name, c2, [base + 256 * c2, base + 256 * c2 + 128]))
    for c2 in range(2):
        g.append(("v_df", c2, [2560 + 256 * c2, 2560 + 256 * c2 + 128]))
    for c in range(4):
        g.append(("lru", c, [3072 + 128 * c, 3584 + 128 * c]))
    for c in range(4):
        g.append(("sc_cx", c, [4608 + 128 * c, 5120 + 128 * c]))
    for c2 in range(2):
        g.append(("sc_b", c2, [4096 + 256 * c2, 4096 + 256 * c2 + 128]))
    return g


WIN_G = win_groups()
NG_IN, NG_OUT, NG_UP, NG_DN = 22, 8, 32, 16


class PPL:
    def __init__(self, L):
        o = 0
        self.g = o; o += L * 64
        self.lcw = o; o += L * 16
        self.lcb = o; o += L * 4
        self.bg = o; o += L * 8
        self.ll = o; o += L * 4
        self.scw = o; o += L * 12
        self.fcw = o; o += L * 192
        self.sub = o; o += L
        self.bfar = o; o += 4
        self.n = o


def build(S, L):
    NT = S // T
    nc = bass.Bass("TRN2", target_bir_lowering=False)
    ppl = PPL(L)
    dt = nc.dram_tensor
    xin = dt("xT", [D, S], F32, kind="ExternalInput").ap()
    w_in = dt("w_in", [L * D, INC], F32, kind="ExternalInput").ap()
    w_out = dt("w_out", [L * D, D], F32, kind="ExternalInput").ap()
    w_up = dt("w_up", [L * D, 2 * DFF], F32, kind="ExternalInput").ap()
    w_dn = dt("w_dn", [L * DFF, D], F32, kind="ExternalInput").ap()
    pp_d = dt("pp", [128, ppl.n], F32, kind="ExternalInput").ap()
    lam_d = dt("lamraw", [128, L * 256], F32, kind="ExternalInput").ap()
    bdg_d = dt("bdg", [128, L * 1024], F32, kind="ExternalInput").ap()
    cst_d = dt("consts", [128, 896], F32, kind="ExternalInput").ap()
    bias_d = dt("biasT", [128, 1024], F32, kind="ExternalInput").ap()
    yout = dt("yT", [D, S], F32, kind="ExternalOutput").ap()
    wg_in = dt("wg_in", [L * NG_IN * 128, 4096], BF16).ap()
    wg_out = dt("wg_out", [L * NG_OUT * 128, 4096], BF16).ap()
    wg_up = dt("wg_up", [L * NG_UP * 128, 4096], BF16).ap()
    wg_dn = dt("wg_dn", [L * NG_DN * 128, 4096], BF16).ap()
    xs = [xin]
    for l in range(1, L):
        xs.append(dt("xmid%d" % l, [D, S], F32).ap())
    xs.append(yout)

    st = contextlib.ExitStack()
    sb = lambda n, w, d=F32: st.enter_context(nc.sbuf_tensor(n, [128, w], d))
    xt = sb("xt", NCH * T)
    regA = sb("regA", NCH * T, BF16)
    mixed = sb("mixed", NCH * T)
    actp = sb("actp", 16 * T)
    act = actp[:, :].bitcast(BF16)
    NRB = 12 * 128
    kr_sb = sb("kr_sb", 4 * NRB, BF16)
    kr_df = sb("kr_df", 4 * NRB, BF16)
    vr_sb = sb("vr_sb", 12 * 512, BF16)
    vr_df = sb("vr_df", 12 * 512, BF16)
    NWB = 3
    wbuf = [sb("wbuf%d" % i, 4096, BF16) for i in range(NWB)]
    rstd = sb("rstd", T)
    qT = mixed[:, 0:2048].bitcast(BF16)
    convm = mixed[:, 2048:4096]
    tA = [mixed[:, 4096:4608]] * 2
    tB = [mixed[:, 4608:5120]] * 2
    tC = [mixed[:, 5120:5632]] * 2
    tD = [mixed[:, 5632:6144]] * 2
    ubg = [mixed[:, 6144:6144 + T + 2]] * 2
    ubu = [mixed[:, 6672:6672 + T + 2]] * 2
    ycs = actp[:, 0:2048].bitcast(BF16)
    Et, tt, cs32, Lp, wt = [], [], [], [], []
    for X in range(2):
        o = 2048 + X * 3072
        Et.append([actp[:, o:o + 512], actp[:, o + 512:o + 1024]])
        tt.append(actp[:, o + 1024:o + 1536])
        cs32.append(actp[:, o + 1536:o + 2048])
        Lp.append([actp[:, o + 2048:o + 2304].bitcast(BF16), actp[:, o + 2304:o + 2560].bitcast(BF16)])
        wt.append([actp[:, o + 2560:o + 2816].bitcast(BF16), actp[:, o + 2816:o + 3072].bitcast(BF16)])
    xcb = [sb("xcb_t", T, BF16)] * 2
    lxb = [sb("lxb_t", T + 4)] * 2
    zeros_bf = sb("zeros_bf", T, BF16)
    nof32 = sb("nof32", 128)
    pp = sb("pp_sb", ppl.n)
    lamr = xt[:, 3072:3072 + L * 256]
    lamt = sb("lamt", 64)
    small = sb("small", 64)
    sp8 = sb("sp8", L * 4)
    sp16 = sb("sp16", L * 4)
    neglam = sb("neglam", L)
    gsl = sb("gsl", L)
    cstf = xt[:, 2048:2048 + 896]
    cbf = sb("cbf", 768, BF16)
    btab = sb("btab", 1024)
    bdgb = sb("bdgb", L * 1024, BF16)
    lru_state = sb("lru_state", 4)
    lru_halo = sb("lru_halo", 16)
    sc_halo = sb("sc_halo", 8)
    ffn_halo = sb("ffn_halo", 128)
    ps = st.enter_context(nc.psum_tensor("ps", [128, 4096], F32))

    P = Prog(nc)
    B = {}

    def bf(name):
        if name not in B:
            B[name] = Buf(name)
        return B[name]

    RA = [bf("regA%d" % c_) for c_ in range(NCH)]
    bank = lambda b: ps[:, b * 512:(b + 1) * 512]
    bbank = lambda b: bf("bank%d" % b)
    negtri = cbf[:, 0:128]
    negones = cbf[:, 128:256]
    ones = cbf[:, 256:384]
    mdiag = cbf[:, 384:512]
    medge = cbf[:, 512:640]
    negtri_i = cbf[:, 640:768]
    col = lambda t_, c: t_[:, c:c + 1]

    P.dma("sp", pp[:], pp_d, writes=[bf("pp")])
    P.dma("sp", lamr[:], lam_d, writes=[bf("lamr")])
    P.dma("sp", cstf[:], cst_d, writes=[bf("cstf")])
    P.dma("sp", btab[:], bias_d, writes=[bf("btab")])
    P.dma("sp", xt[:, 0:L * 1024], bdg_d, writes=[bf("xt")])
    P.op("dve", lambda e: e.tensor_copy(out=cbf[:], in_=cstf[:, 0:768]), reads=[bf("cstf")], writes=[bf("cbf")])
    P.op("pool", lambda e: e.memset(zeros_bf[:], 0.0), writes=[bf("zeros")])
    P.op("pool", lambda e: e.memset(nof32[:], -1.0), writes=[bf("nof32")])
    P.op("dve", lambda e: e.tensor_copy(out=bdgb[:], in_=xt[:, 0:L * 1024]), reads=[bf("xt")], writes=[bf("bdgb")])
    for h in range(4):
        P.op("dve", lambda e, h=h: e.tensor_tensor(out=btab[:, h * 256:h * 256 + 128], in0=btab[:, h * 256:h * 256 + 128],
                                                   in1=cstf[:, 768:896], op=ALU.add),
             reads=[bf("cstf"), bf("btab")], writes=[bf("btab")])
    for l in range(L):
        lam_init = 0.8 - 0.6 * math.exp(-0.3 * l)
        b0 = l * 256
        P.op("dve", lambda e, b0=b0: e.tensor_tensor(out=lamt[:, 0:64], in0=lamr[:, b0:b0 + 64], in1=lamr[:, b0 + 64:b0 + 128], op=ALU.mult),
             reads=[bf("lamr")], writes=[bf("lamt")])
        P.op("dve", lambda e: e.reduce_sum(out=small[:, 0:1], in_=lamt[:, 0:64], axis=mybir.AxisListType.X),
             reads=[bf("lamt")], writes=[bf("small")])
        P.op("dve", lambda e, b0=b0: e.tensor_tensor(out=lamt[:, 0:64], in0=lamr[:, b0 + 128:b0 + 192], in1=lamr[:, b0 + 192:b0 + 256], op=ALU.mult),
             reads=[bf("lamr"), bf("small")], writes=[bf("lamt")])
        P.op("dve", lambda e: e.reduce_sum(out=small[:, 1:2], in_=lamt[:, 0:64], axis=mybir.AxisListType.X),
             reads=[bf("lamt")], writes=[bf("small")])
        P.op("act", lambda e: e.activation(out=small[:, 2:4], in_=small[:, 0:2], func=AF.Exp), reads=[bf("small")], writes=[bf("small")])
        P.op("dve", lambda e: e.tensor_tensor(out=small[:, 4:5], in0=small[:, 3:4], in1=small[:, 2:3], op=ALU.subtract),
             reads=[bf("small")], writes=[bf("small")])
        P.op("dve", lambda e, l=l, li=lam_init: e.tensor_scalar(out=neglam[:, l:l + 1], in0=small[:, 4:5], scalar1=-li, scalar2=None, op0=ALU.add),
             reads=[bf("small")], writes=[bf("neglam")])
        P.op("dve", lambda e, l=l, li=lam_init: e.tensor_scalar(out=gsl[:, l:l + 1], in0=pp[:, ppl.sub + l:ppl.sub + l + 1], scalar1=1.0 - li, scalar2=None, op0=ALU.mult),
             reads=[bf("pp")], writes=[bf("gsl")])
    n4 = L * 4
    lamc = pp[:, ppl.ll:ppl.ll + n4]
    s_ = lambda i: small[:, 8 + i * n4:8 + (i + 1) * n4]
    bs = bf("small2")
    P.op("dve", lambda e: e.tensor_scalar(out=s_(0), in0=lamc, scalar1=-1.0, scalar2=None, op0=ALU.mult), reads=[bf("pp")], writes=[bs])
    P.op("dve", lambda e: e.tensor_tensor(out=s_(1), in0=lamc, in1=s_(0), op=ALU.min), reads=[bf("pp"), bs], writes=[bs])
    P.op("act", lambda e: e.activation(out=s_(2), in_=s_(1), func=AF.Exp), reads=[bs], writes=[bs])
    P.op("dve", lambda e: e.tensor_scalar(out=s_(3), in0=s_(2), scalar1=2.0, scalar2=None, op0=ALU.add), reads=[bs], writes=[bs])
    P.op("dve", lambda e: e.reciprocal(out=s_(3), in_=s_(3)), reads=[bs], writes=[bs])
    P.op("dve", lambda e: e.tensor_tensor(out=s_(3), in0=s_(3), in1=s_(2), op=ALU.mult), reads=[bs], writes=[bs])
    P.op("dve", lambda e: e.tensor_tensor(out=s_(4), in0=s_(3), in1=s_(3), op=ALU.mult), reads=[bs], writes=[bs])
    P.op("dve", lambda e: e.tensor_scalar(out=s_(5), in0=s_(4), scalar1=1.0 / 11, scalar2=1.0 / 9, op0=ALU.mult, op1=ALU.add), reads=[bs], writes=[bs])
    for cst in (1.0 / 7, 1.0 / 5, 1.0 / 3, 1.0):
        P.op("dve", lambda e: e.tensor_tensor(out=s_(5), in0=s_(5), in1=s_(4), op=ALU.mult), reads=[bs], writes=[bs])
        P.op("dve", lambda e, cst=cst: e.tensor_scalar(out=s_(5), in0=s_(5), scalar1=cst, scalar2=None, op0=ALU.add), reads=[bs], writes=[bs])
    P.op("dve", lambda e: e.tensor_tensor(out=s_(5), in0=s_(5), in1=s_(3), op=ALU.mult), reads=[bs], writes=[bs])
    P.op("dve", lambda e: e.tensor_scalar(out=s_(0), in0=s_(0), scalar1=0.0, scalar2=None, op0=ALU.max), reads=[bs], writes=[bs])
    P.op("dve", lambda e: e.scalar_tensor_tensor(out=s_(1), in0=s_(5), scalar=2.0, in1=s_(0), op0=ALU.mult, op1=ALU.add), reads=[bs], writes=[bs])
    P.op("dve", lambda e: e.tensor_scalar(out=sp8[:], in0=s_(1), scalar1=-8.0, scalar2=None, op0=ALU.mult), reads=[bs], writes=[bf("sp8")])
    P.op("dve", lambda e: e.tensor_scalar(out=sp16[:], in0=s_(1), scalar1=-16.0, scalar2=None, op0=ALU.mult), reads=[bs], writes=[bf("sp8")])
    P.barrier()

    cast_engs = ("act", "dve", "pool")
    ucount = [0]

    def cast_unit(src_ap, dst_ap, n, cw):
        u = ucount[0]
        ucount[0] += 1
        s = u % 8
        fs = mixed[:, s * 1024:s * 1024 + n]
        fb = act[:, s * 1024:s * 1024 + n]
        P.dma("sp", fs.rearrange("p (k c) -> p k c", c=cw), src_ap, writes=[bf("stg%d" % s)])
        eng = cast_engs[u % 3]
        if eng == "act":
            P.op("act", lambda e: e.activation(out=fb, in_=fs, func=AF.Copy), reads=[bf("stg%d" % s)], writes=[bf("stb%d" % s)])
        else:
            P.op(eng, lambda e: e.tensor_copy(out=fb, in_=fs), reads=[bf("stg%d" % s)], writes=[bf("stb%d" % s)])
        P.dma("pool", dst_ap, fb.rearrange("p (k c) -> p k c", c=cw), reads=[bf("stb%d" % s)], writes=[bf("wscr")])

    for l in range(L):
        for gi, (kind, idx, cols) in enumerate(WIN_G):
            for kq in range(4):
                for pi, c0 in enumerate(cols):
                    src = w_in[l * D + kq * 512:l * D + (kq + 1) * 512, c0:c0 + 128].rearrange("(k p) c -> p k c", p=128)
                    r0 = (l * NG_IN + gi) * 128
                    dst = wg_in[r0:r0 + 128, :].rearrange("p (k c) -> p k c", c=256)[:, kq * 4:(kq + 1) * 4, pi * 128:(pi + 1) * 128]
                    cast_unit(src, dst, 512, 128)
        for gi in range(NG_OUT):
            for kq in range(4):
                src = w_out[l * D + kq * 512:l * D + (kq + 1) * 512, gi * 256:(gi + 1) * 256].rearrange("(k p) c -> p k c", p=128)
                r0 = (l * NG_OUT + gi) * 128
                dst = wg_out[r0:r0 + 128, kq * 1024:(kq + 1) * 1024].rearrange("p (k c) -> p k c", c=256)
                cast_unit(src, dst, 1024, 256)
        for gi in range(NG_UP):
            for kq in range(4):
                for pi, c0 in enumerate((gi * 128, DFF + gi * 128)):
                    src = w_up[l * D + kq * 512:l * D + (kq + 1) * 512, c0:c0 + 128].rearrange("(k p) c -> p k c", p=128)
                    r0 = (l * NG_UP + gi) * 128
                    dst = wg_up[r0:r0 + 128, :].rearrange("p (k c) -> p k c", c=256)[:, kq * 4:(kq + 1) * 4, pi * 128:(pi + 1) * 128]
                    cast_unit(src, dst, 512, 128)
        for gi in range(NG_DN):
            for kq in range(4):
                src = w_dn[l * DFF + kq * 1024:l * DFF + (kq + 1) * 1024, gi * 128:(gi + 1) * 128].rearrange("(k p) c -> p k c", p=128)
                r0 = (l * NG_DN + gi) * 128
                dst = wg_dn[r0:r0 + 128, kq * 1024:(kq + 1) * 1024].rearrange("p (k c) -> p k c", c=128)
                cast_unit(src, dst, 1024, 128)
    P.barrier()

    wctr = [0]
    dbank = [0]

    def load_w(src_rows):
        i = wctr[0] % NWB
        wctr[0] += 1
        P.dma("sp", wbuf[i][:], src_rows, reads=[bf("wscr")], writes=[bf("wbuf%d" % i)])
        return wbuf[i], bf("wbuf%d" % i)

    def next_bank():
        b = dbank[0] % 4
        dbank[0] += 1
        return b

    def mm(out, lhsT, rhs, start, stop, reads, writes):
        P.op("pe", lambda e: e.matmul(out, lhsT=lhsT, rhs=rhs, start=start, stop=stop), reads=reads, writes=writes)

    def rmsnorm_stats(src, nfeat_chunks, src_bufs):
        for q4 in range(nfeat_chunks // 4):
            P.op("act", lambda e, q4=q4: e.activation(out=regA[:, q4 * 4 * T:(q4 + 1) * 4 * T], in_=src[:, q4 * 4 * T:(q4 + 1) * 4 * T], func=AF.Square),
                 reads=(src_bufs[q4] if isinstance(src_bufs[0], list) else src_bufs), writes=RA[q4 * 4:(q4 + 1) * 4])
        for c in range(nfeat_chunks):
            mm(bank(7), ones, regA[:, c * T:(c + 1) * T], c == 0, c == nfeat_chunks - 1, [bf("cbf"), RA[c]], [bbank(7)])
        P.op("act", lambda e: e.activation(out=rstd[:], in_=bank(7), func=AF.Sqrt, bias=EPS, scale=1.0 / (128 * nfeat_chunks)),
             reads=[bbank(7)], writes=[bf("rstd")])
        P.op("dve", lambda e: e.reciprocal(out=rstd[:], in_=rstd[:]), reads=[bf("rstd")], writes=[bf("rstd")])

    def key_blocks(i0):
        out = []
        for j in range(i0 + 3, max(0, i0 - 8) - 1, -1):
            ia = max(j, i0)
            ib = min(j + 8, i0 + 3)
            diag = j - i0 if i0 <= j <= i0 + 3 else None
            edge = j + 8 - i0 if i0 <= j + 8 <= i0 + 3 else None
            out.append((j, (ia - i0) * 128, (ib - i0 + 1) * 128, diag, edge))
        return out

    def ring_pos(j):
        return ((j // 4) % 3) * 4 + (j % 4)

    def ringbuf(name, j):
        return bf("%s_s%d" % (name, (j // 4) % 3))

    def body(l, t):
        i0 = 4 * t
        slot = t % 3
        gc = lambda n, c: col(pp, ppl.g + (l * 4 + n) * 16 + c)
        bxt = bf("xt")
        if PROFILE:
            P.scope = "A_load_norm1_%d_%d" % (l, t)
        if bxt.kids is None:
            bxt.kids = [bf("xtq%d" % q_) for q_ in range(4)]
        xsrc = xs[l].rearrange("(c p) s -> p c s", p=128)[:, :, t * T:(t + 1) * T]
        xdst_sb = xt[:].rearrange("p (c s) -> p c s", s=T)
        for q_ in range(4):
            P.dma("pool", xdst_sb[:, q_ * 4:(q_ + 1) * 4, :], xsrc[:, q_ * 4:(q_ + 1) * 4, :],
                  reads=[bf("x%d_%d" % (l, t))] if l > 0 else [], writes=[bxt.kids[q_]])
        if t == 0:
            for nm, tn in (("lru_state", lru_state), ("lru_halo", lru_halo), ("sc_halo", sc_halo), ("ffn_halo", ffn_halo)):
                P.op("pool", lambda e, tn=tn: e.memset(tn[:], 0.0), writes=[bf(nm)])
        rmsnorm_stats(xt, NCH, [[k_] for k_ in bxt.kids])
        for c in range(NCH):
            P.op("dve", lambda e, c=c: e.scalar_tensor_tensor(out=regA[:, c * T:(c + 1) * T], in0=xt[:, c * T:(c + 1) * T], scalar=gc(0, c),
                                                              in1=rstd[:], op0=ALU.mult, op1=ALU.mult),
                 reads=[bxt, bf("rstd"), bf("pp")], writes=[RA[c]])
        if PROFILE:
            P.scope = "C_win_%d_%d" % (l, t)
        for gi, (kind, idx, cols) in enumerate(WIN_G):
            r0 = (l * NG_IN + gi) * 128
            wb, wbb = load_w(wg_in[r0:r0 + 128, :])
            if kind in ("v_sb", "v_df"):
                ring = vr_sb if kind == "v_sb" else vr_df
                rname = "vr_sb" if kind == "v_sb" else "vr_df"
                for tb in range(4):
                    b = next_bank()
                    for kc in range(NCH):
                        mm(ps[:, b * 512:b * 512 + 256], regA[:, kc * T + tb * 128:kc * T + (tb + 1) * 128], wb[:, kc * 256:(kc + 1) * 256],
                           kc == 0, kc == NCH - 1, [RA[kc], wbb], [bbank(b)])
                    rb = slot * 4 + tb
                    dst = ring[:, rb * 512 + idx * 256:rb * 512 + (idx + 1) * 256]
                    eng = "act" if tb % 2 == 0 else "dve"
                    if eng == "act":
                        P.op("act", lambda e, b=b, dst=dst: e.activation(out=dst, in_=ps[:, b * 512:b * 512 + 256], func=AF.Copy),
                             reads=[bbank(b)], writes=[bf("%s_s%d" % (rname, slot))])
                    else:
                        P.op("dve", lambda e, b=b, dst=dst: e.tensor_copy(out=dst, in_=ps[:, b * 512:b * 512 + 256]),
                             reads=[bbank(b)], writes=[bf("%s_s%d" % (rname, slot))])
                continue
            banks = []
            for ci in range(2):
                b = next_bank()
                banks.append(b)
                for kc in range(NCH):
                    mm(bank(b), wb[:, kc * 256 + ci * 128:kc * 256 + (ci + 1) * 128], regA[:, kc * T:(kc + 1) * T],
                       kc == 0, kc == NCH - 1, [RA[kc], wbb], [bbank(b)])
            if kind in ("q_sb", "q_df", "k_sb", "k_df"):
                for ci in range(2):
                    c = idx * 2 + ci
                    b = banks[ci]
                    if kind == "q_sb":
                        dst, wbuf_ = qT[:, c * T:(c + 1) * T], bf("qT")
                    elif kind == "q_df":
                        dst, wbuf_ = qT[:, (4 + c) * T:(5 + c) * T], bf("qT")
                    elif kind == "k_sb":
                        dst, wbuf_ = kr_sb[:, c * NRB + slot * 512:c * NRB + (slot + 1) * 512], bf("kr_sb_s%d" % slot)
                    else:
                        dst, wbuf_ = kr_df[:, c * NRB + slot * 512:c * NRB + (slot + 1) * 512], bf("kr_df_s%d" % slot)
                    if ci == 0:
                        P.op("act", lambda e, b=b, dst=dst: e.activation(out=dst, in_=bank(b), func=AF.Copy), reads=[bbank(b)], writes=[wbuf_])
                    else:
                        P.op("dve", lambda e, b=b, dst=dst: e.tensor_copy(out=dst, in_=bank(b)), reads=[bbank(b)], writes=[wbuf_])
            elif kind == "lru":
                c = idx
                r = 0
                bx_, bg_ = banks
                lx = lxb[r]
                blx = bf("lxb%d" % r)
                P.op("pool", lambda e, lx=lx, c=c: e.tensor_copy(out=lx[:, 0:3], in_=lru_halo[:, c * 4:c * 4 + 3]), reads=[bf("lru_halo")], writes=[blx])
                P.op("act", lambda e, lx=lx, b=bx_: e.activation(out=lx[:, 3:3 + T], in_=bank(b), func=AF.Copy), reads=[bbank(bx_)], writes=[blx])
                P.op("pool", lambda e, lx=lx, c=c: e.tensor_copy(out=lru_halo[:, c * 4:c * 4 + 3], in_=lx[:, T:T + 3]), reads=[blx], writes=[bf("lru_halo")])
                xc = tA[r]
                bxc = bf("tA%d" % r)
                wk = lambda k: col(pp, ppl.lcw + (l * 4 + k) * 4 + c)
                P.op("dve", lambda e, lx=lx, xc=xc: e.tensor_scalar(out=xc[:], in0=lx[:, 3:3 + T], scalar1=wk(3), scalar2=col(pp, ppl.lcb + l * 4 + c),
                                                                    op0=ALU.mult, op1=ALU.add), reads=[blx, bf("pp")], writes=[bxc])
                for k in range(3):
                    P.op("dve", lambda e, lx=lx, xc=xc, k=k: e.scalar_tensor_tensor(out=xc[:], in0=lx[:, k:k + T], scalar=wk(k), in1=xc[:],
                                                                                    op0=ALU.mult, op1=ALU.add), reads=[blx, bxc, bf("pp")], writes=[bxc])
                P.op("pool", lambda e, xc=xc, r=r: e.tensor_copy(out=xcb[r][:], in_=xc[:]), reads=[bxc], writes=[bf("xcb%d" % r)])
                gb = []
                for g in range(2):
                    b = next_bank()
                    gb.append(b)
                    off = ((l * 2 + g) * 4 + c) * 128
                    mm(bank(b), bdgb[:, off:off + 128], xcb[r][:], True, True, [bf("bdgb"), bf("xcb%d" % r)], [bbank(b)])
                rr, ii = tB[r], tC[r]
                P.op("act", lambda e, rr=rr, b=gb[0]: e.activation(out=rr[:], in_=bank(b), func=AF.Sigmoid, bias=col(pp, ppl.bg + (l * 2 + 0) * 4 + c)),
                     reads=[bbank(gb[0]), bf("pp")], writes=[bf("tB%d" % r)])
                P.op("act", lambda e, ii=ii, b=gb[1]: e.activation(out=ii[:], in_=bank(b), func=AF.Sigmoid, bias=col(pp, ppl.bg + (l * 2 + 1) * 4 + c)),
                     reads=[bbank(gb[1]), bf("pp")], writes=[bf("tC%d" % r)])
                P.op("dve", lambda e, ii=ii, xc=xc: e.tensor_tensor(out=ii[:], in0=ii[:], in1=xc[:], op=ALU.mult), reads=[bf("tC%d" % r), bxc], writes=[bf("tC%d" % r)])
                a2 = tD[r]
                P.op("act", lambda e, a2=a2, rr=rr: e.activation(out=a2[:], in_=rr[:], func=AF.Exp, scale=col(sp16, l * 4 + c)),
                     reads=[bf("tB%d" % r), bf("sp8")], writes=[bf("tD%d" % r)])
                P.op("act", lambda e, a2=a2: e.activation(out=a2[:], in_=a2[:], func=AF.Sqrt, bias=1.0, scale=-1.0), reads=[bf("tD%d" % r)], writes=[bf("tD%d" % r)])
                P.op("act", lambda e, rr=rr: e.activation(out=rr[:], in_=rr[:], func=AF.Exp, scale=col(sp8, l * 4 + c)),
                     reads=[bf("tB%d" % r), bf("sp8")], writes=[bf("tB%d" % r)])
                P.op("dve", lambda e, ii=ii, a2=a2: e.tensor_tensor(out=ii[:], in0=ii[:], in1=a2[:], op=ALU.mult), reads=[bf("tC%d" % r), bf("tD%d" % r)], writes=[bf("tC%d" % r)])
                P.op("dve", lambda e, xc=xc, rr=rr, ii=ii: e.tensor_tensor_scan(out=xc[:], data0=rr[:], data1=ii[:], initial=lru_state[:, c:c + 1],
                                                                                op0=ALU.mult, op1=ALU.add),
                     reads=[bf("tB%d" % r), bf("tC%d" % r), bf("lru_state")], writes=[bxc])
                P.op("dve", lambda e, xc=xc: e.tensor_copy(out=lru_state[:, c:c + 1], in_=xc[:, T - 1:T]), reads=[bxc], writes=[bf("lru_state")])
                P.op("act", lambda e, a2=a2, b=bg_: e.activation(out=a2[:], in_=bank(b), func=AF.Gelu_apprx_tanh), reads=[bbank(bg_)], writes=[bf("tD%d" % r)])
                P.op("dve", lambda e, xc=xc, a2=a2: e.tensor_tensor(out=ycs[:, c * T:(c + 1) * T], in0=xc[:], in1=a2[:], op=ALU.mult),
                     reads=[bxc, bf("tD%d" % r)], writes=[bf("ycs")])
            elif kind == "sc_cx":
                c = idx
                r = 0
                bc_, bx_ = banks
                tmp = tA[r]
                mb = lxb[r]
                bmb = bf("lxb%d" % r)
                P.op("act", lambda e, tmp=tmp, b=bc_: e.activation(out=tmp[:], in_=bank(b), func=AF.Copy), reads=[bbank(bc_)], writes=[bf("tA%d" % r)])
                P.op("pool", lambda e, mb=mb: e.tensor_copy(out=mb[:, 0:2], in_=sc_halo[:, c * 2:c * 2 + 2]), reads=[bf("sc_halo")], writes=[bmb])
                P.op("dve", lambda e, mb=mb, tmp=tmp, b=bx_: e.tensor_tensor(out=mb[:, 2:2 + T], in0=tmp[:], in1=bank(b), op=ALU.mult),
                     reads=[bf("tA%d" % r), bbank(bx_)], writes=[bmb])
                P.op("pool", lambda e, mb=mb: e.tensor_copy(out=sc_halo[:, c * 2:c * 2 + 2], in_=mb[:, T:T + 2]), reads=[bmb], writes=[bf("sc_halo")])
                wk = lambda k: col(pp, ppl.scw + (l * 3 + k) * 4 + c)
                cm = convm[:, c * T:(c + 1) * T]
                P.op("dve", lambda e, mb=mb, cm=cm: e.tensor_scalar(out=cm, in0=mb[:, 2:2 + T], scalar1=wk(2), scalar2=None, op0=ALU.mult),
                     reads=[bmb, bf("pp")], writes=[bf("convm")])
                for k in range(2):
                    P.op("dve", lambda e, mb=mb, cm=cm, k=k: e.scalar_tensor_tensor(out=cm, in0=mb[:, k:k + T], scalar=wk(k), in1=cm, op0=ALU.mult, op1=ALU.add),
                         reads=[bmb, bf("convm"), bf("pp")], writes=[bf("convm")])
            elif kind == "sc_b":
                for ci in range(2):
                    c = idx * 2 + ci
                    b = banks[ci]
                    P.op("dve", lambda e, b=b, c=c: e.tensor_tensor(out=ycs[:, (4 + c) * T:(5 + c) * T], in0=convm[:, c * T:(c + 1) * T], in1=bank(b), op=ALU.mult),
                         reads=[bf("convm"), bbank(b)], writes=[bf("ycs")])

        if PROFILE:
            P.scope = "D_sb_%d_%d" % (l, t)
        kbl = key_blocks(i0)
        nb = len(kbl)
        zq = [0]

        def znext():
            zq[0] += 1
            return (0, 1, 5, 6)[zq[0] % 4]

        for c in range(4):
            for X in range(2):
                P.op("pool", lambda e, X=X: e.memset(cs32[X][:], 0.0), writes=[bf("cs32_%d" % X)])
                pbx = X * 64
                mm(ps[pbx:pbx + 64, 4 * 512:5 * 512], ones[:, 0:64], zeros_bf[:], True, False, [bf("zeros"), bf("cbf")], [bf("o_sb%d" % X), bbank(4)])
            zbank = {}

            def stage1a(X, n):
                j, ca, cb, diag, edge = kbl[n]
                pb = X * 64
                r = n % 2
                zb = znext()
                kcol = c * NRB + ring_pos(j) * 128
                mm(ps[:, zb * 512 + ca:zb * 512 + cb], kr_sb[pb:pb + 64, kcol:kcol + 128], qT[pb:pb + 64, c * T + ca:c * T + cb], True, True,
                   [ringbuf("kr_sb", j), bf("qT")], [bbank(zb)])
                P.op("act", lambda e: e.activation(out=Et[X][r][:, ca:cb], in_=ps[:, zb * 512 + ca:zb * 512 + cb], func=AF.Exp, scale=0.125),
                     reads=[bbank(zb)], writes=[bf("Et%d_%d" % (X, r))])

            def stage1b(X, n):
                j, ca, cb, diag, edge = kbl[n]
                r = n % 2
                P.op("act", lambda e: e.activation(out=Lp[X][r][:, ca:cb], in_=Et[X][r][:, ca:cb], func=AF.Ln, bias=1.0),
                     reads=[bf("Et%d_%d" % (X, r))], writes=[bf("Lp%d_%d" % (X, r))])
                for sub, m in ((diag, mdiag), (edge, medge)):
                    if sub is not None:
                        P.op("dve", lambda e, sub=sub, m=m: e.tensor_tensor(out=Lp[X][r][:, sub * 128:(sub + 1) * 128], in0=Lp[X][r][:, sub * 128:(sub + 1) * 128], in1=m, op=ALU.mult),
                             reads=[bf("Lp%d_%d" % (X, r)), bf("cbf")], writes=[bf("Lp%d_%d" % (X, r))])

            def stage2a(X, n):
                j, ca, cb, diag, edge = kbl[n]
                r = n % 2
                sbk = 2 + X
                mm(ps[:, sbk * 512 + ca:sbk * 512 + cb], negtri_i, Lp[X][r][:, ca:cb], True, n == 0, [bf("cbf"), bf("Lp%d_%d" % (X, r))], [bbank(sbk)])
                if n > 0:
                    mm(ps[:, sbk * 512 + ca:sbk * 512 + cb], nof32[:], cs32[X][:, ca:cb], False, True, [bf("nof32"), bf("cs32_%d" % X)], [bbank(sbk)])
                P.op("act", lambda e: e.activation(out=tt[X][:, ca:cb], in_=ps[:, sbk * 512 + ca:sbk * 512 + cb], func=AF.Exp),
                     reads=[bbank(sbk)], writes=[bf("tt%d" % X)])

            def stage2b(X, n):
                j, ca, cb, diag, edge = kbl[n]
                r = n % 2
                P.op("dve", lambda e: e.tensor_tensor(out=wt[X][r][:, ca:cb], in0=Et[X][r][:, ca:cb], in1=tt[X][:, ca:cb], op=ALU.mult),
                     reads=[bf("Et%d_%d" % (X, r)), bf("tt%d" % X)], writes=[bf("wt%d_%d" % (X, r))])
                for sub, m in ((diag, mdiag), (edge, medge)):
                    if sub is not None:
                        P.op("dve", lambda e, sub=sub, m=m: e.tensor_tensor(out=wt[X][r][:, sub * 128:(sub + 1) * 128], in0=wt[X][r][:, sub * 128:(sub + 1) * 128], in1=m, op=ALU.mult),
                             reads=[bf("wt%d_%d" % (X, r)), bf("cbf")], writes=[bf("wt%d_%d" % (X, r))])
                if n < nb - 1:
                    P.op("pool", lambda e: e.tensor_tensor(out=cs32[X][:, ca:cb], in0=cs32[X][:, ca:cb], in1=Lp[X][r][:, ca:cb], op=ALU.add),
                         reads=[bf("cs32_%d" % X), bf("Lp%d_%d" % (X, r))], writes=[bf("cs32_%d" % X)])

            def stage3(X, n):
                j, ca, cb, diag, edge = kbl[n]
                r = n % 2
                pb = X * 64
                hd = 2 * c + X
                vcol = ring_pos(j) * 512 + hd * 64
                mm(ps[pb:pb + 64, 4 * 512 + ca:4 * 512 + cb], vr_sb[:, vcol:vcol + 64], wt[X][r][:, ca:cb], False, n == nb - 1,
                   [ringbuf("vr_sb", j), bf("wt%d_%d" % (X, r))], [bf("o_sb%d" % X)])

            for n in range(nb + 2):
                if n < nb:
                    for X in range(2):
                        stage1a(X, n)
                    for X in range(2):
                        stage1b(X, n)
                if 1 <= n <= nb:
                    for X in range(2):
                        stage2a(X, n - 1)
                    for X in range(2):
                        stage2b(X, n - 1)
                if n >= 2:
                    for X in range(2):
                        stage3(X, n - 2)
            P.op("act", lambda e, c=c: e.activation(out=regA[:, c * T:(c + 1) * T], in_=bank(4), func=AF.Copy),
                 reads=[bf("o_sb0"), bf("o_sb1"), bbank(4)], writes=[RA[c]])
        if PROFILE:
            P.scope = "D_df_%d_%d" % (l, t)
        for h in range(4):
            obk = (2, 3)
            dbk = (4, 7)
            for X in range(2):
                mm(bank(obk[X]), ones, zeros_bf[:], True, False, [bf("zeros"), bf("cbf")], [bbank(obk[X])])
                mm(bank(dbk[X]), ones, zeros_bf[:], True, False, [bf("zeros"), bf("cbf")], [bbank(dbk[X])])
            zbank = {}

            def dstage1(X, n):
                j, ca, cb, diag, edge = kbl[n]
                pb = X * 64
                r = n % 2
                zb = znext()
                kcol = h * NRB + ring_pos(j) * 128
                mm(ps[:, zb * 512 + ca:zb * 512 + cb], kr_df[pb:pb + 64, kcol:kcol + 128], qT[pb:pb + 64, (4 + h) * T + ca:(4 + h) * T + cb], True, True,
                   [ringbuf("kr_df", j), bf("qT")], [bbank(zb)])
                near = []
                for sub in range(ca // 128, cb // 128):
                    d = i0 + sub - j
                    if d in (0, 1):
                        near.append((sub, d))
                fa, fb = ca, cb
                for sub, d in near:
                    s0, s1 = sub * 128, (sub + 1) * 128
                    P.op("dve", lambda e, s0=s0, s1=s1, d=d: e.scalar_tensor_tensor(out=tt[X][:, s0:s1], in0=ps[:, zb * 512 + s0:zb * 512 + s1], scalar=0.125,
                                                                                     in1=btab[:, h * 256 + d * 128:h * 256 + (d + 1) * 128], op0=ALU.mult, op1=ALU.add),
                         reads=[bbank(zb), bf("btab")], writes=[bf("tt%d" % X)])
                    P.op("act", lambda e, s0=s0, s1=s1: e.activation(out=wt[X][r][:, s0:s1], in_=tt[X][:, s0:s1], func=AF.Exp),
                         reads=[bf("tt%d" % X)], writes=[bf("wt%d_%d" % (X, r))])
                    if s0 == fa:
                        fa = s1
                    elif s1 == fb:
                        fb = s0
                if fb > fa:
                    P.op("act", lambda e, fa=fa, fb=fb: e.activation(out=wt[X][r][:, fa:fb], in_=ps[:, zb * 512 + fa:zb * 512 + fb], func=AF.Exp, scale=0.125,
                                                                    bias=col(pp, ppl.bfar + h)),
                         reads=[bbank(zb), bf("pp")], writes=[bf("wt%d_%d" % (X, r))])
                if edge is not None:
                    P.op("pool", lambda e, edge=edge: e.memset(wt[X][r][0:64, edge * 128 + 64:edge * 128 + 128], 0.0), writes=[bf("wt%d_%d" % (X, r))])

            def dstage2(X, n):
                j, ca, cb, diag, edge = kbl[n]
                r = n % 2
                vcol = ring_pos(j) * 512 + h * 128
                mm(ps[:, obk[X] * 512 + ca:obk[X] * 512 + cb], vr_df[:, vcol:vcol + 128], wt[X][r][:, ca:cb], False, n == nb - 1,
                   [ringbuf("vr_df", j), bf("wt%d_%d" % (X, r))], [bbank(obk[X])])
                mm(ps[:, dbk[X] * 512 + ca:dbk[X] * 512 + cb], ones, wt[X][r][:, ca:cb], False, n == nb - 1,
                   [bf("cbf"), bf("wt%d_%d" % (X, r))], [bbank(dbk[X])])

            if DBG_SEQ:
                for X in range(2):
                    for n in range(nb + 1):
                        if n < nb:
                            dstage1(X, n)
                        if n >= 1:
                            dstage2(X, n - 1)
            else:
                for n in range(nb + 1):
                    for X in range(2):
                        if n < nb:
                            dstage1(X, n)
                    for X in range(2):
                        if n >= 1:
                            dstage2(X, n - 1)
            a0, a1, r0_, r1_ = tA[0], tB[0], tC[0], tD[0]
            P.op("dve", lambda e: e.reciprocal(out=r0_[:], in_=bank(4)), reads=[bbank(4)], writes=[bf("tC0")])
            P.op("dve", lambda e: e.tensor_tensor(out=a0[:], in0=r0_[:], in1=bank(2), op=ALU.mult), reads=[bf("tC0"), bbank(2)], writes=[bf("tA0")])
            P.op("dve", lambda e: e.reciprocal(out=r1_[:], in_=bank(7)), reads=[bbank(7)], writes=[bf("tD0")])
            P.op("dve", lambda e: e.tensor_tensor(out=a1[:], in0=r1_[:], in1=bank(3), op=ALU.mult), reads=[bf("tD0"), bbank(3)], writes=[bf("tB0")])
            P.op("dve", lambda e: e.scalar_tensor_tensor(out=a0[:], in0=a1[:], scalar=col(neglam, l), in1=a0[:], op0=ALU.mult, op1=ALU.add),
                 reads=[bf("tA0"), bf("tB0"), bf("neglam")], writes=[bf("tA0")])
            P.op("act", lambda e: e.activation(out=xcb[0][:], in_=a0[:], func=AF.Square), reads=[bf("tA0")], writes=[bf("xcb0")])
            mm(bank(7), ones, xcb[0][:], True, True, [bf("cbf"), bf("xcb0")], [bbank(7)])
            P.op("act", lambda e: e.activation(out=r0_[:], in_=bank(7), func=AF.Sqrt, bias=EPS, scale=1.0 / 128), reads=[bbank(7)], writes=[bf("tC0")])
            P.op("dve", lambda e: e.reciprocal(out=r0_[:], in_=r0_[:]), reads=[bf("tC0")], writes=[bf("tC0")])
            P.op("dve", lambda e, h=h: e.scalar_tensor_tensor(out=regA[:, (4 + h) * T:(5 + h) * T], in0=a0[:], scalar=col(gsl, l), in1=r0_[:], op0=ALU.mult, op1=ALU.mult),
                 reads=[bf("tA0"), bf("tC0"), bf("gsl")], writes=[RA[4 + h]])
        if PROFILE:
            P.scope = "E_wout_%d_%d" % (l, t)
        P.fence([bf(n) for n in ("qT", "convm", "tA0", "tB0", "tC0", "tD0", "ubg0", "ubu0")])
        for gi in range(NG_OUT):
            r0 = (l * NG_OUT + gi) * 128
            wb, wbb = load_w(wg_out[r0:r0 + 128, :])
            for ci in range(2):
                b = next_bank()
                mc = gi * 2 + ci
                for kc in range(NCH):
                    rhs = regA[:, kc * T:(kc + 1) * T] if kc < 8 else ycs[:, (kc - 8) * T:(kc - 7) * T]
                    mm(bank(b), wb[:, kc * 256 + ci * 128:kc * 256 + (ci + 1) * 128], rhs, kc == 0, kc == NCH - 1,
                       [RA[kc] if kc < 8 else bf("ycs"), wbb], [bbank(b)])
                if ci == 0:
                    P.op("act", lambda e, b=b, mc=mc: e.activation(out=mixed[:, mc * T:(mc + 1) * T], in_=bank(b), func=AF.Copy), reads=[bbank(b)], writes=[bf("mixed")])
                else:
                    P.op("dve", lambda e, b=b, mc=mc: e.tensor_copy(out=mixed[:, mc * T:(mc + 1) * T], in_=bank(b)), reads=[bbank(b)], writes=[bf("mixed")])

        def post_norm_residual(nidx):
            rmsnorm_stats(mixed, NCH, [bf("mixed")])
            for c in range(NCH):
                P.op("dve", lambda e, c=c: e.scalar_tensor_tensor(out=mixed[:, c * T:(c + 1) * T], in0=mixed[:, c * T:(c + 1) * T], scalar=gc(nidx, c),
                                                                  in1=rstd[:], op0=ALU.mult, op1=ALU.mult),
                     reads=[bf("mixed"), bf("rstd"), bf("pp")], writes=[bf("mixed")])
            P.op("dve", lambda e: e.tensor_tensor(out=xt[:], in0=xt[:], in1=mixed[:], op=ALU.add), reads=[bxt, bf("mixed")], writes=[bxt])

        post_norm_residual(1)
        if PROFILE:
            P.scope = "F_norm2_%d_%d" % (l, t)
        P.fence([bf("mixed")])
        P.fence([bf(n) for n in ("ycs", "Et0_0", "Et0_1", "Et1_0", "Et1_1", "tt0", "tt1", "cs32_0", "cs32_1", "Lp0_0", "Lp0_1", "Lp1_0", "Lp1_1", "wt0_0", "wt0_1", "wt1_0", "wt1_1", "xcb0", "lxb0")])
        rmsnorm_stats(xt, NCH, [bxt])
        for c in range(NCH):
            P.op("dve", lambda e, c=c: e.scalar_tensor_tensor(out=regA[:, c * T:(c + 1) * T], in0=xt[:, c * T:(c + 1) * T], scalar=gc(2, c),
                                                              in1=rstd[:], op0=ALU.mult, op1=ALU.mult),
                 reads=[bxt, bf("rstd"), bf("pp")], writes=[RA[c]])
        if PROFILE:
            P.scope = "F_wup_%d_%d" % (l, t)
        for gi in range(NG_UP):
            r0 = (l * NG_UP + gi) * 128
            wb, wbb = load_w(wg_up[r0:r0 + 128, :])
            r = 0
            bks = []
            for ci in range(2):
                b = next_bank()
                bks.append(b)
                for kc in range(NCH):
                    mm(bank(b), wb[:, kc * 256 + ci * 128:kc * 256 + (ci + 1) * 128], regA[:, kc * T:(kc + 1) * T], kc == 0, kc == NCH - 1,
                       [RA[kc], wbb], [bbank(b)])
            outs = []
            for ci, (ub, nm, ch) in enumerate(((ubg[r], "ubg%d" % r, gi), (ubu[r], "ubu%d" % r, 32 + gi))):
                b = bks[ci]
                P.op("pool", lambda e, ub=ub, ch=ch: e.tensor_copy(out=ub[:, 0:2], in_=ffn_halo[:, ch * 2:ch * 2 + 2]), reads=[bf("ffn_halo")], writes=[bf(nm)])
                P.op("act", lambda e, ub=ub, b=b: e.activation(out=ub[:, 2:2 + T], in_=bank(b), func=AF.Copy), reads=[bbank(b)], writes=[bf(nm)])
                P.op("pool", lambda e, ub=ub, ch=ch: e.tensor_copy(out=ffn_halo[:, ch * 2:ch * 2 + 2], in_=ub[:, T:T + 2]), reads=[bf(nm)], writes=[bf("ffn_halo")])
                cv = tC[r] if ci == 0 else tD[r]
                cvn = ("tC%d" if ci == 0 else "tD%d") % r
                wk = lambda k, ch=ch: col(pp, ppl.fcw + (l * 3 + k) * 64 + ch)
                P.op("dve", lambda e, ub=ub, cv=cv, wk=wk: e.tensor_scalar(out=cv[:], in0=ub[:, 2:2 + T], scalar1=wk(2), scalar2=None, op0=ALU.mult),
                     reads=[bf(nm), bf("pp")], writes=[bf(cvn)])
                for k in range(2):
                    P.op("dve", lambda e, ub=ub, cv=cv, wk=wk, k=k: e.scalar_tensor_tensor(out=cv[:], in0=ub[:, k:k + T], scalar=wk(k), in1=cv[:], op0=ALU.mult, op1=ALU.add),
                         reads=[bf(nm), bf(cvn), bf("pp")], writes=[bf(cvn)])
                outs.append((cv, cvn))
            (cg, cgn), (cu, cun) = outs
            P.op("act", lambda e, cg=cg: e.activation(out=cg[:], in_=cg[:], func=AF.Gelu_apprx_tanh), reads=[bf(cgn)], writes=[bf(cgn)])
            P.op("dve", lambda e, cg=cg, cu=cu, gi=gi: e.tensor_tensor(out=act[:, gi * T:(gi + 1) * T], in0=cg[:], in1=cu[:], op=ALU.mult),
                 reads=[bf(cgn), bf(cun)], writes=[bf("act")])
        P.fence([bf(n) for n in ("qT", "convm", "tA0", "tB0", "tC0", "tD0", "ubg0", "ubu0")])
        if PROFILE:
            P.scope = "F_wdn_%d_%d" % (l, t)
        for gi in range(NG_DN):
            r0 = (l * NG_DN + gi) * 128
            wb, wbb = load_w(wg_dn[r0:r0 + 128, :])
            b = next_bank()
            for kc in range(32):
                mm(bank(b), wb[:, kc * 128:(kc + 1) * 128], act[:, kc * T:(kc + 1) * T], kc == 0, kc == 31, [bf("act"), wbb], [bbank(b)])
            if gi % 2 == 0:
                P.op("act", lambda e, b=b, gi=gi: e.activation(out=mixed[:, gi * T:(gi + 1) * T], in_=bank(b), func=AF.Copy), reads=[bbank(b)], writes=[bf("mixed")])
            else:
                P.op("dve", lambda e, b=b, gi=gi: e.tensor_copy(out=mixed[:, gi * T:(gi + 1) * T], in_=bank(b)), reads=[bbank(b)], writes=[bf("mixed")])
        if PROFILE:
            P.scope = "G_norm_store_%d_%d" % (l, t)
        post_norm_residual(3)
        P.fence([bf("mixed"), bf("act")])
        xdst = xs[l + 1].rearrange("(c p) s -> p c s", p=128)[:, :, t * T:(t + 1) * T]
        P.dma("pool", xdst, xt[:].rearrange("p (c s) -> p c s", s=T), reads=[bxt], writes=[bf("x%d_%d" % (l + 1, t))], final=(l == L - 1))

    for l in range(L):
        for t in range(NT):
            body(l, t)
    P.emit()
    st.close()
    return nc


def _t5_bucket_np(rel):
    nb = 16
    max_exact = 8
    ret = np.where(rel > 0, nb, 0)
    n = np.abs(rel)
    nf = np.maximum(n, 1).astype(np.float32) / np.float32(max_exact)
    large = max_exact + (np.log(nf).astype(np.float32) / np.float32(math.log(128 / max_exact)) * np.float32(nb - max_exact)).astype(np.int32)
    large = np.minimum(large, nb - 1)
    return ret + np.where(n < max_exact, n, large)


def _prep_shared(inp, L):
    f = lambda a: np.ascontiguousarray(np.asarray(a, dtype=np.float32))
    ppl = PPL(L)
    pp = np.zeros((128, ppl.n), np.float32)
    cm = lambda a: np.asarray(a, np.float32).reshape(-1, 128).T
    pp[:, ppl.g:ppl.g + L * 64] = cm(inp["norm_gains"])
    pp[:, ppl.lcw:ppl.lcw + L * 16] = cm(inp["lru_conv_w"])
    pp[:, ppl.lcb:ppl.lcb + L * 4] = cm(inp["lru_conv_b"])
    pp[:, ppl.bg:ppl.bg + L * 8] = cm(inp["lru_b_gate"])
    pp[:, ppl.ll:ppl.ll + L * 4] = cm(inp["lru_lambda"])
    pp[:, ppl.scw:ppl.scw + L * 12] = cm(inp["sc_conv_w"])
    pp[:, ppl.fcw:ppl.fcw + L * 192] = cm(inp["ffn_conv_w"])
    pp[:, ppl.sub:ppl.sub + L] = cm(inp["diff_subln_g"])
    rb = np.asarray(inp["rel_bias"], np.float32)
    pp[:, ppl.bfar:ppl.bfar + 4] = np.broadcast_to(rb[15][None, :], (128, 4))
    lamraw = np.ascontiguousarray(np.broadcast_to(np.asarray(inp["diff_lambda"], np.float32).reshape(1, L * 256), (128, L * 256)))
    wg = np.asarray(inp["lru_w_gate"], np.float32)
    bdg = np.zeros((128, L * 1024), np.float32)
    for l in range(L):
        for g in range(2):
            for c in range(4):
                off = ((l * 2 + g) * 4 + c) * 128
                bdg[0:64, off:off + 64] = wg[l, g, 2 * c]
                bdg[64:128, off + 64:off + 128] = wg[l, g, 2 * c + 1]
    s_ = np.arange(128)[:, None]
    q_ = np.arange(128)[None, :]
    consts = np.zeros((128, 896), np.float32)
    consts[:, 0:128] = -(s_ > q_).astype(np.float32)
    consts[:, 128:256] = -1.0
    consts[:, 256:384] = 1.0
    consts[:, 384:512] = (s_ < q_).astype(np.float32)
    consts[:, 512:640] = (s_ >= q_).astype(np.float32)
    consts[:, 640:768] = -(s_ >= q_).astype(np.float32)
    consts[:, 768:896] = np.where((s_ >= 64) & (q_ < 64), NEG, 0.0)
    biasT = np.zeros((128, 1024), np.float32)
    for d in range(2):
        bk = _t5_bucket_np(s_ - q_ - 128 * d)
        for h in range(4):
            biasT[:, h * 256 + d * 128:h * 256 + (d + 1) * 128] = rb[bk, h]
    return dict(
        w_in=f(inp["w_in"]).reshape(L * D, INC), w_out=f(inp["w_out"]).reshape(L * D, D),
        w_up=f(inp["ffn_w_up"]).reshape(L * D, 2 * DFF), w_dn=f(inp["ffn_w_down"]).reshape(L * DFF, D),
        pp=pp, lamraw=lamraw, bdg=bdg, consts=consts, biasT=biasT)


def run(inputs, L=None):
    x = np.asarray(inputs["x"], np.float32)
    Bn, S, _ = x.shape
    if L is None:
        L = int(np.asarray(inputs["w_in"]).shape[0])
    shared = _prep_shared(inputs, L)
    nc = build(S, L)
    in_maps = []
    for b in range(Bn):
        m = dict(shared)
        m["xT"] = np.ascontiguousarray(x[b].T)
        in_maps.append(m)
    res = run_bass_kernel_spmd(nc, in_maps, core_ids=list(range(Bn)))
    out = np.stack([np.ascontiguousarray(res.results[b]["yT"].T) for b in range(Bn)], axis=0)
    return out.astype(np.float32)


def kernel(**inputs):
    return run(inputs)
```
